# Optimizing a Trainium2 kernel written in Bass

```python
import jax
import jax.numpy as jnp
from jax import lax
import numpy as np

D_MODEL = 1024
BATCH = 8
SEQ = 2048
DEPTH = 4

CTX_LEN = 256
GRID_W = 64
RMS_EPS = 1e-6
GLA_HEADS = 4
GLA_DK = 64
GLA_DV = 128
GLA_GATE_RANK = 16
GLA_GATE_NORM = 16.0
GLA_CHUNK = 64
SG_GROUPS = 4
SG_DIM = 128
SG_CHUNK = 128
FT_GROUPS = 4
FT_DIM = 64
MLA_HEADS = 6
MLA_NOPE = 128
MLA_ROPE = 64
MLA_V = 128
MLA_Q_RANK = 384
MLA_KV_RANK = 256
ROPE_BASE = 10000.0
ATTN_BLOCK = 128
MIX_W = GLA_HEADS * GLA_DV + SG_GROUPS * SG_DIM
FFN_HIDDEN = -(-8 * D_MODEL // (3 * 256)) * 256
EV_SIZES = (GLA_HEADS * GLA_DK, GLA_HEADS * GLA_DK, GLA_HEADS * GLA_DV, GLA_HEADS * GLA_DV,
            GLA_GATE_RANK, GLA_GATE_RANK, SG_GROUPS * SG_DIM, SG_GROUPS * SG_DIM)
EV_IN_W = 2 * GLA_HEADS * GLA_DK + 2 * GLA_HEADS * GLA_DV + 2 * GLA_GATE_RANK + 2 * SG_GROUPS * SG_DIM
OD_SIZES = (FT_GROUPS * FT_DIM, MLA_Q_RANK, MLA_KV_RANK, MLA_ROPE)
OD_IN_W = FT_GROUPS * FT_DIM + MLA_Q_RANK + MLA_KV_RANK + MLA_ROPE
OD_KV_COL0 = FT_GROUPS * FT_DIM + MLA_Q_RANK

kernel_name = "hybrid_gla_gmlp_fnet_mla_flow_block"


def rms_norm(x, g):
    xf = x.astype(jnp.float32)
    y = xf * lax.rsqrt(jnp.mean(xf * xf, axis=-1, keepdims=True) + RMS_EPS)
    return (y * g.astype(jnp.float32)).astype(x.dtype)


def split_cols(p, sizes):
    out, start = [], 0
    for s in sizes:
        out.append(p[..., start:start + s])
        start += s
    return out


def split_mod(m):
    return [t[:, None, :] for t in jnp.split(m, 6, axis=-1)]


def modulate(h, shift, scale):
    return h * (1.0 + scale) + shift


def swiglu(h, w_in, w_out):
    gu = h @ w_in
    return (jax.nn.silu(gu[..., :FFN_HIDDEN]) * gu[..., FFN_HIDDEN:]) @ w_out


def axial_rope(n_tok):
    rows = n_tok // GRID_W
    row_id = jnp.repeat(jnp.arange(rows, dtype=jnp.float32), GRID_W)
    col_id = jnp.tile(jnp.arange(GRID_W, dtype=jnp.float32), rows)
    axis_dim = MLA_ROPE // 2
    inv_freq = ROPE_BASE ** (-jnp.arange(0, axis_dim, 2, dtype=jnp.float32) / axis_dim)
    ang_r = row_id[:, None] * inv_freq
    ang_c = col_id[:, None] * inv_freq
    ang = jnp.concatenate([ang_r, ang_r, ang_c, ang_c], axis=-1)
    return jnp.cos(ang), jnp.sin(ang)


def apply_axial_rope(t, cos, sin):
    quarter = MLA_ROPE // 4
    blocks = t.reshape(*t.shape[:-1], 2, 2, quarter)
    rot = jnp.stack([-blocks[..., 1, :], blocks[..., 0, :]], axis=-2).reshape(t.shape)
    return t * cos[:, None, :] + rot * sin[:, None, :]


def rope_tail(t, rope):
    cos, sin = rope
    cos, sin = cos.astype(t.dtype), sin.astype(t.dtype)
    return jnp.concatenate([t[..., :MLA_NOPE], apply_axial_rope(t[..., MLA_NOPE:], cos, sin)], axis=-1)


def gla_chunk_scan(q, k, v, g, s0):
    B_, L, H, _ = q.shape
    dv = v.shape[-1]
    n = L // GLA_CHUNK

    def to_chunks(t):
        return t.reshape(B_, n, GLA_CHUNK, H, t.shape[-1]).transpose(1, 0, 3, 2, 4)

    causal = jnp.tril(jnp.ones((GLA_CHUNK, GLA_CHUNK), dtype=bool))

    def step(S, inp):
        qi, ki, vi, gi = inp
        b = jnp.cumsum(gi, axis=2)
        o_inter = jnp.einsum('bhtk,bhkv->bhtv', qi * jnp.exp(b), S)
        diff = b[:, :, :, None, :] - b[:, :, None, :, :]
        decay = jnp.exp(jnp.where(causal[:, :, None], diff, -jnp.inf))
        a = jnp.einsum('bhtk,bhsk,bhtsk->bhts', qi, ki, decay)
        o = o_inter + jnp.einsum('bhts,bhsv->bhtv', a, vi)
        b_last = b[:, :, -1:, :]
        S_new = jnp.exp(b_last[:, :, 0, :])[..., None] * S + jnp.einsum('bhsk,bhsv->bhkv', ki * jnp.exp(b_last - b), vi)
        return S_new, o

    S_fin, oc = lax.scan(step, s0, (to_chunks(q), to_chunks(k), to_chunks(v), to_chunks(g)))
    return oc.transpose(1, 0, 3, 2, 4).reshape(B_, L, H, dv), S_fin


def gla_prepare(q, k, v, a_f, a_b, wa_f, ba_f, wa_b, ba_b):
    B_, L, _ = q.shape
    f32 = jnp.float32

    def hd(t, d):
        return t.astype(f32).reshape(B_, L, GLA_HEADS, d)

    def log_decay(a, w, b):
        return hd(jax.nn.log_sigmoid((a @ w + b).astype(f32)) / GLA_GATE_NORM, GLA_DK)

    return (hd(q, GLA_DK) * GLA_DK ** -0.5, hd(k, GLA_DK), hd(v, GLA_DV),
            log_decay(a_f, wa_f, ba_f), log_decay(a_b, wa_b, ba_b))


def gla_bidirectional(con, lat):
    qc, kc, vc, gfc, gbc = con
    qx, kx, vx, gfx, gbx = lat
    s0 = jnp.zeros((qc.shape[0], GLA_HEADS, GLA_DK, GLA_DV), jnp.float32)
    rev = lambda t: t[:, ::-1]
    oc_f, sc_f = gla_chunk_scan(qc, kc, vc, gfc, s0)
    ox_f, _ = gla_chunk_scan(qx, kx, vx, gfx, sc_f)
    oc_b, sc_b = gla_chunk_scan(rev(qc), rev(kc), rev(vc), rev(gbc), s0)
    ox_b, _ = gla_chunk_scan(rev(qx), rev(kx), rev(vx), rev(gbx), sc_b)
    return oc_f + rev(oc_b), ox_f + rev(ox_b)


def gla_output(o, g_out, onorm_g):
    B_, L = o.shape[:2]
    on = rms_norm(o, onorm_g).reshape(B_, L, GLA_HEADS * GLA_DV).astype(g_out.dtype)
    return on * jax.nn.silu(g_out)


def spatial_gating(u, v, vnorm_g, ws, bs):
    B_, L, _ = v.shape
    n = L // SG_CHUNK
    u = jax.nn.gelu(u, approximate=False)
    vn = rms_norm(jax.nn.gelu(v, approximate=False).reshape(B_, L, SG_GROUPS, SG_DIM), vnorm_g)
    vc = vn.reshape(B_, n, SG_CHUNK, SG_GROUPS, SG_DIM)
    mixed = jnp.einsum('gts,bnsgc->bntgc', ws, vc) + bs.T[:, :, None]
    return u * mixed.reshape(B_, L, SG_GROUPS * SG_DIM)


def fourier_mix(h):
    B_, L, _ = h.shape
    hg = h.reshape(B_, L, FT_GROUPS, FT_DIM).astype(jnp.float32)
    out = jnp.fft.fft2(hg, axes=(1, 3), norm='ortho').real
    return out.astype(h.dtype).reshape(B_, L, FT_GROUPS * FT_DIM)


def mla_queries(qa, qa_g, wuq, qn_g, rope):
    B_, L, _ = qa.shape
    q = (rms_norm(qa, qa_g) @ wuq).reshape(B_, L, MLA_HEADS, MLA_NOPE + MLA_ROPE)
    q = rms_norm(q, qn_g)
    return q if rope is None else rope_tail(q, rope)


def mla_keys_values(kva, kpe, kva_g, wukv, kn_g, rope):
    B_, L, _ = kva.shape
    kv = (rms_norm(kva, kva_g) @ wukv).reshape(B_, L, MLA_HEADS, MLA_NOPE + MLA_V)
    k_pe = jnp.broadcast_to(kpe[:, :, None, :], (B_, L, MLA_HEADS, MLA_ROPE))
    k = rms_norm(jnp.concatenate([kv[..., :MLA_NOPE], k_pe], axis=-1), kn_g)
    k = k if rope is None else rope_tail(k, rope)
    return k, kv[..., MLA_NOPE:]


def block_attention(q, k, v):
    B_, Lq, H, dq = q.shape
    dv = v.shape[-1]
    nb = Lq // ATTN_BLOCK
    qb = q.reshape(B_, nb, ATTN_BLOCK, H, dq).transpose(1, 0, 2, 3, 4)
    scale = dq ** -0.5

    def one_block(qi):
        s = jnp.einsum('bqhd,bkhd->bhqk', qi, k, preferred_element_type=jnp.float32) * scale
        p = jax.nn.softmax(s, axis=-1).astype(v.dtype)
        return jnp.einsum('bhqk,bkhd->bqhd', p, v)

    ob = lax.map(one_block, qb)
    return ob.transpose(1, 0, 2, 3, 4).reshape(B_, Lq, H * dv)


def even_mixer(zx, zc, w_in, wa_f, ba_f, wa_b, ba_b, onorm_g, vnorm_g, ws, bs, need_ctx):
    qx, kx, vx, gx, afx, abx, ux, svx = split_cols(zx @ w_in, EV_SIZES)
    qc, kc, vc, gc, afc, abc, uc, svc = split_cols(zc @ w_in, EV_SIZES)
    lat = gla_prepare(qx, kx, vx, afx, abx, wa_f, ba_f, wa_b, ba_b)
    con = gla_prepare(qc, kc, vc, afc, abc, wa_f, ba_f, wa_b, ba_b)
    o_c, o_x = gla_bidirectional(con, lat)
    mx = jnp.concatenate([gla_output(o_x, gx, onorm_g), spatial_gating(ux, svx, vnorm_g, ws, bs)], axis=-1)
    mc = None
    if need_ctx:
        mc = jnp.concatenate([gla_output(o_c, gc, onorm_g), spatial_gating(uc, svc, vnorm_g, ws, bs)], axis=-1)
    return mx, mc


def odd_mixer(zx, zc, w_in, qa_g, wuq, kva_g, wukv, qn_g, kn_g, rope, need_ctx):
    ftx, qax, kvax, kpex = split_cols(zx @ w_in, OD_SIZES)
    qx = mla_queries(qax, qa_g, wuq, qn_g, rope)
    kx, vx = mla_keys_values(kvax, kpex, kva_g, wukv, kn_g, rope)
    if need_ctx:
        ftc, qac, kvac, kpec = split_cols(zc @ w_in, OD_SIZES)
    else:
        kvac, kpec = split_cols(zc @ w_in[:, OD_KV_COL0:], OD_SIZES[2:])
    kc, vc = mla_keys_values(kvac, kpec, kva_g, wukv, kn_g, None)
    att_x = block_attention(qx, jnp.concatenate([kc, kx], axis=1), jnp.concatenate([vc, vx], axis=1))
    mx = jnp.concatenate([fourier_mix(ftx), att_x], axis=-1)
    mc = None
    if need_ctx:
        qc = mla_queries(qac, qa_g, wuq, qn_g, None)
        mc = jnp.concatenate([fourier_mix(ftc), block_attention(qc, kc, vc)], axis=-1)
    return mx, mc


def setup_inputs(seed: int = 0) -> dict:
    key = jax.random.key(seed)
    keys = jax.random.split(key, 32)
    counter = iter(range(32))
    f32 = jnp.float32

    def nrm(shape, s):
        return jax.random.normal(keys[next(counter)], shape, f32) * s

    def gain(shape):
        return 1.0 + nrm(shape, 0.02)

    D = D_MODEL
    NE = (DEPTH + 1) // 2
    NO = DEPTH // 2
    return {
        "x": nrm((BATCH, SEQ, D), 1.0),
        "c": nrm((BATCH, D), 1.0),
        "ctx": nrm((BATCH, CTX_LEN, D), 1.0),
        "c_ctx": nrm((D,), 1.0),
        "ada_w": nrm((DEPTH, D, 6 * D), 0.5 * D ** -0.5),
        "ada_b": nrm((DEPTH, 6 * D), 0.02),
        "norm_mix_g": gain((DEPTH, D)),
        "norm_ffn_g": gain((DEPTH, D)),
        "w_mix_out": nrm((DEPTH, MIX_W, D), MIX_W ** -0.5),
        "ffn_w_in": nrm((DEPTH, D, 2 * FFN_HIDDEN), D ** -0.5),
        "ffn_w_out": nrm((DEPTH, FFN_HIDDEN, D), FFN_HIDDEN ** -0.5),
        "ev_w_in": nrm((NE, D, EV_IN_W), D ** -0.5),
        "gla_wa_f": nrm((NE, GLA_GATE_RANK, GLA_HEADS * GLA_DK), GLA_GATE_RANK ** -0.5),
        "gla_ba_f": nrm((NE, GLA_HEADS * GLA_DK), 0.1),
        "gla_wa_b": nrm((NE, GLA_GATE_RANK, GLA_HEADS * GLA_DK), GLA_GATE_RANK ** -0.5),
        "gla_ba_b": nrm((NE, GLA_HEADS * GLA_DK), 0.1),
        "gla_onorm_g": gain((NE, GLA_DV)),
        "sg_vnorm_g": gain((NE, SG_GROUPS, SG_DIM)),
        "sg_ws": nrm((NE, SG_GROUPS, SG_CHUNK, SG_CHUNK), SG_CHUNK ** -0.5),
        "sg_bs": gain((NE, SG_GROUPS, SG_CHUNK)),
        "od_w_in": nrm((NO, D, OD_IN_W), D ** -0.5),
        "mla_qa_g": gain((NO, MLA_Q_RANK)),
        "mla_wuq": nrm((NO, MLA_Q_RANK, MLA_HEADS * (MLA_NOPE + MLA_ROPE)), MLA_Q_RANK ** -0.5),
        "mla_kva_g": gain((NO, MLA_KV_RANK)),
        "mla_wukv": nrm((NO, MLA_KV_RANK, MLA_HEADS * (MLA_NOPE + MLA_V)), MLA_KV_RANK ** -0.5),
        "mla_qn_g": gain((NO, MLA_NOPE + MLA_ROPE)),
        "mla_kn_g": gain((NO, MLA_NOPE + MLA_ROPE)),
    }


def reference(x, c, ctx, c_ctx, ada_w, ada_b, norm_mix_g, norm_ffn_g, w_mix_out, ffn_w_in, ffn_w_out,
              ev_w_in, gla_wa_f, gla_ba_f, gla_wa_b, gla_ba_b, gla_onorm_g, sg_vnorm_g, sg_ws, sg_bs,
              od_w_in, mla_qa_g, mla_wuq, mla_kva_g, mla_wukv, mla_qn_g, mla_kn_g):
    rope = axial_rope(x.shape[1])
    silu_c = jax.nn.silu(c)
    silu_cc = jax.nn.silu(c_ctx)[None, :]
    h = ctx
    for l in range(DEPTH):
        need_ctx = l < DEPTH - 1
        sm_x, cm_x, gm_x, sf_x, cf_x, gf_x = split_mod(silu_c @ ada_w[l] + ada_b[l])
        sm_c, cm_c, gm_c, sf_c, cf_c, gf_c = split_mod(silu_cc @ ada_w[l] + ada_b[l])
        zx = modulate(rms_norm(x, norm_mix_g[l]), sm_x, cm_x)
        zc = modulate(rms_norm(h, norm_mix_g[l]), sm_c, cm_c)
        i = l // 2
        if l % 2 == 0:
            mx, mc = even_mixer(zx, zc, ev_w_in[i], gla_wa_f[i], gla_ba_f[i], gla_wa_b[i], gla_ba_b[i],
                                gla_onorm_g[i], sg_vnorm_g[i], sg_ws[i], sg_bs[i], need_ctx)
        else:
            mx, mc = odd_mixer(zx, zc, od_w_in[i], mla_qa_g[i], mla_wuq[i], mla_kva_g[i], mla_wukv[i],
                               mla_qn_g[i], mla_kn_g[i], rope, need_ctx)
        x = x + gm_x * (mx @ w_mix_out[l])
        x = x + gf_x * swiglu(modulate(rms_norm(x, norm_ffn_g[l]), sf_x, cf_x), ffn_w_in[l], ffn_w_out[l])
        if need_ctx:
            h = h + gm_c * (mc @ w_mix_out[l])
            h = h + gf_c * swiglu(modulate(rms_norm(h, norm_ffn_g[l]), sf_c, cf_c), ffn_w_in[l], ffn_w_out[l])
    return x
```

```python
import numpy as np
import ml_dtypes
from contextlib import ExitStack
import concourse.bass as bass
import concourse.mybir as mybir
from concourse.bass_utils import run_bass_kernel_spmd

F32 = mybir.dt.float32
BF16 = mybir.dt.bfloat16
AF = mybir.ActivationFunctionType
ALU = mybir.AluOpType

D = 1024
SEQ = 2048
CTX = 256
T = SEQ + CTX
NT = T // 128
O1W = 1
DEPTH = 4
FFN_H = 2816
EPS = 1e-6
TBS = [(0, 256)] + [(256 + 512 * i, 512) for i in range(4)]
ENGS = ["sync", "scalar", "vector", "gpsimd", "tensor"]


def tb_of_tile(t):
    return 0 if t < 2 else 1 + (t - 2) // 4


class _Op:
    __slots__ = ("eng", "fn", "deps", "sig", "num", "is_dma", "dkey", "dcum")


class Prog:
    def __init__(self, nc):
        self.nc = nc
        self.ops = []
        self.w = {}
        self.r = {}
        self.dma_cnt = {}
        self.last = {e: None for e in ENGS}
        self.pending_bar = {e: [] for e in ENGS}
        self.unwaited_dma = []

    def add(self, eng, fn, reads=(), writes=(), dkey=None):
        op = _Op()
        op.eng = eng
        op.fn = fn
        op.sig = False
        op.num = 0
        op.is_dma = dkey is not None
        op.dkey = dkey
        deps = {}
        for k in reads:
            for o in self.w.get(k, {}).values():
                deps[id(o)] = o
        for k in writes:
            for o in self.w.get(k, {}).values():
                deps[id(o)] = o
            for o in self.r.get(k, {}).values():
                deps[id(o)] = o
        for o in self.pending_bar[eng]:
            deps[id(o)] = o
        self.pending_bar[eng] = []
        dl = []
        for o in deps.values():
            if o is op:
                continue
            if (not o.is_dma) and o.eng == eng and eng == "tensor":
                continue
            if o.is_dma:
                dl.append((o, self.dma_cnt[o.dkey] * 16))
            else:
                dl.append((o, None))
        op.deps = dl
        if op.is_dma:
            self.dma_cnt[dkey] = self.dma_cnt.get(dkey, 0) + 1
            op.dcum = self.dma_cnt[dkey] * 16
        slot = id(op) if op.is_dma else eng
        for k in reads:
            self.r.setdefault(k, {})[slot] = op
        for k in writes:
            self.w[k] = {slot: op}
            self.r[k] = {}
        self.ops.append(op)
        if not op.is_dma:
            self.last[eng] = op
        else:
            self.unwaited_dma.append(op)
        return op

    def barrier(self):
        lasts = [o for o in self.last.values() if o is not None]
        dmas = list(self.unwaited_dma)
        self.unwaited_dma = []
        for e in ENGS:
            self.pending_bar[e] = self.pending_bar[e] + [o for o in lasts if o.eng != e or e != "tensor"] + dmas

    def emit(self, final_dma_keys):
        nc = self.nc
        for op in self.ops:
            for (o, _) in op.deps:
                if not o.is_dma:
                    o.sig = True
        cnt = {e: 0 for e in ENGS}
        for op in self.ops:
            if (not op.is_dma) and op.sig:
                cnt[op.eng] += 1
                op.num = cnt[op.eng]
        by_eng = {e: [o for o in self.ops if o.eng == e] for e in ENGS}
        with ExitStack() as es:
            esem = {e: es.enter_context(nc.semaphore("se_" + e)) for e in ENGS}
            dsem = {k: es.enter_context(nc.semaphore("sd_%d" % i)) for i, k in enumerate(self.dma_cnt)}
            block = es.enter_context(nc.Block())

            def make(eng):
                def body(e):
                    waited = {}
                    for op in by_eng[eng]:
                        for (o, v) in op.deps:
                            if o.is_dma:
                                sem, val, key = dsem[o.dkey], v, ("d", o.dkey)
                            else:
                                sem, val, key = esem[o.eng], o.num, ("e", o.eng)
                            if waited.get(key, 0) < val:
                                e.wait_ge(sem, val)
                                waited[key] = val
                        inst = op.fn(e)
                        if op.is_dma:
                            inst.then_inc(dsem[op.dkey], 16)
                        elif op.sig:
                            inst.then_inc(esem[eng], 1)
                    if eng == "sync":
                        for k in final_dma_keys:
                            e.wait_ge(dsem[k], self.dma_cnt[k] * 16)
                return body

            for eng in ENGS:
                getattr(block, eng)(make(eng))


class _Stop(Exception):
    pass


def build_program(n_layers=DEPTH, dbg=None, stop=None):
    nc = bass.Bass("TRN2", target_bir_lowering=False)
    P = Prog(nc)

    def din(name, shape, dt=F32):
        return nc.dram_tensor(name, list(shape), dt, kind="ExternalInput").ap()

    x_d = din("x", [SEQ, D])
    ctx_d = din("ctx", [CTX, D])
    cvec_d = din("cvec", [128, 8, 2])
    ada_w_d = din("ada_w", [DEPTH, D, 6 * D])
    ada_b_d = din("ada_b_fm", [DEPTH, 128, 48])
    ng_d = din("ng_fm", [DEPTH, 128, 16])
    wmo_d = din("w_mix_out", [DEPTH, D, D])
    fwi_d = din("ffn_w_in", [DEPTH, D, 2 * FFN_H])
    fwo_d = din("ffn_w_out", [DEPTH, FFN_H, D])
    evw_d = din("ev_w_in", [2, D, 2592])
    wab_d = din("gla_wab", [2, 33, 2, 256])
    gon_d = din("gla_on_fm", [2, 128, 1])
    vng_d = din("sg_vng_bc", [2, 128, 512])
    wst_d = din("sg_wsT", [2, 128, 4, 128])
    sbs_d = din("sg_bs", [2, 1, 512])
    odw_d = din("od_w_in", [2, D, 960])
    qag_d = din("qa_g_fm", [2, 128, 3])
    kvg_d = din("kva_g_fm", [2, 128, 2])
    qkn_d = din("qkn_g_fm", [2, 128, 4])
    wuq_d = din("mla_wuq", [2, 384, 1152])
    wukv_d = din("mla_wukv", [2, 256, 1536])
    ident_d = din("ident", [128, 128])
    tri_d = din("tri", [128, 6, 128])
    rope_d = din("rope_cs", [64, 2, SEQ])
    rmat_d = din("rmat", [64, 64])
    c64_d = din("c64blk", [128, 256])
    dftL_d = din("dft2048", [16, 128, 2, SEQ], BF16)
    dftC_d = din("dft256", [2, 128, 2, CTX], BF16)
    out_d = nc.dram_tensor("out", [SEQ, D], F32, kind="ExternalOutput").ap()
    dbg_d = {}
    if dbg:
        for name, shape in dbg.items():
            dbg_d[name] = nc.dram_tensor("dbg_" + name, list(shape), F32, kind="ExternalOutput").ap()

    top = ExitStack()
    a_base = (nc.sbuf_base + 31) // 32 * 32
    a_size = nc.sbuf_top - a_base - 2048
    nc.alloc_sbuf_tensor("arena", [128, a_size], mybir.dt.uint8)
    arena = {"cur": a_base, "n": 0, "peak": 0}

    def _rel(mark):
        arena["cur"] = mark

    def sb(es, name, shape, dt):
        if not hasattr(es, "_mark"):
            es._mark = arena["cur"]
            es.callback(_rel, es._mark)
        nb = int(np.prod(shape[1:])) * (4 if dt == F32 else 2)
        nb = (nb + 63) // 64 * 64
        off = arena["cur"]
        assert off + nb <= a_base + a_size, ("SBUF arena overflow", name, off + nb - a_base, a_size)
        arena["cur"] = off + nb
        if arena["cur"] - a_base > arena["peak"]:
            arena["peak"] = arena["cur"] - a_base
            arena["peak_name"] = name
        arena["n"] += 1
        return nc.alloc_sbuf_tensor_at("%s_%d" % (name, arena["n"]), list(shape), dt, offset=off)

    ST = sb(top, "ST", [128, 8, T], F32)
    Z = sb(top, "Z", [128, 8, T], BF16)
    ident = sb(top, "ident", [128, 128], F32)
    ones_bf = sb(top, "ones_bf", [128, 128], BF16)
    silu_c = sb(top, "silu_c", [128, 8, 2], BF16)
    MOD = [sb(top, "MOD%d" % i, [128, 48, 2], F32) for i in range(2)]
    AB = sb(top, "AB", [128, 2, 8, 2], F32)
    ng = sb(top, "ng", [128, DEPTH, 16], F32)
    adab = sb(top, "adab", [128, DEPTH, 48], F32)
    PS = [top.enter_context(nc.psum_tensor("ps%d" % i, [128, 512], F32)) for i in range(8)]
    ps_rr = [0]

    ps_held = set()

    def ps_next(hold=False):
        for _ in range(8):
            i = ps_rr[0]
            ps_rr[0] = (i + 1) % 8
            if i not in ps_held:
                if hold:
                    ps_held.add(i)
                return i
        raise RuntimeError("all PSUM banks held")

    def ps_release(i):
        ps_held.discard(i)

    def dma(eng, out, in_, reads, writes, dkey):
        return P.add(eng, lambda e: e.dma_start(out=out, in_=in_), reads, writes, dkey=dkey)

    def mm(out, lhsT, rhs, start, stop, reads, writes):
        return P.add("tensor", lambda e: e.matmul(out, lhsT=lhsT, rhs=rhs, start=start, stop=stop), reads, writes)

    def tr(out, in_, reads, writes):
        return P.add("tensor", lambda e: e.transpose(out, in_, ident[:]), list(reads) + ["ident"], writes)

    def act(out, in_, func, reads, writes, bias=None, scale=None, accum_out=None):
        kw = {}
        if bias is not None:
            kw["bias"] = bias
        if scale is not None:
            kw["scale"] = scale
        if accum_out is not None:
            kw["accum_out"] = accum_out
        return P.add("scalar", lambda e: e.activation(out=out, in_=in_, func=func, **kw), reads, writes)

    def vtt(out, in0, in1, op, reads, writes, eng="vector"):
        return P.add(eng, lambda e: e.tensor_tensor(out=out, in0=in0, in1=in1, op=op), reads, writes)

    def vstt(out, in0, scalar, in1, op0, op1, reads, writes):
        return P.add("vector", lambda e: e.scalar_tensor_tensor(out=out, in0=in0, scalar=scalar, in1=in1,
                                                                 op0=op0, op1=op1), reads, writes)

    def vts(out, in0, s1, s2, op0, op1, reads, writes, eng="vector"):
        if op1 is None:
            return P.add(eng, lambda e: e.tensor_scalar(out=out, in0=in0, scalar1=s1, scalar2=None, op0=op0),
                         reads, writes)
        return P.add(eng, lambda e: e.tensor_scalar(out=out, in0=in0, scalar1=s1, scalar2=s2, op0=op0, op1=op1),
                     reads, writes)

    def vcopy(out, in_, reads, writes, eng="vector"):
        return P.add(eng, lambda e: e.tensor_copy(out=out, in_=in_), reads, writes)

    def vrecip(out, in_, reads, writes):
        return P.add("vector", lambda e: e.reciprocal(out=out, in_=in_), reads, writes)

    def memset(eng, ap, val, writes):
        return P.add(eng, lambda e: e.memset(ap, val), [], writes)

    def rsqrt_from(out_f32, in_ap, scale, reads, writes, tmpkey=None):
        act(out_f32, in_ap, AF.Ln, reads, writes, bias=eps_t[:out_f32.shape[0], :], scale=scale)
        act(out_f32, out_f32, AF.Exp, writes, writes, scale=-0.5)

    def recip_act(out_f32, in_ap, reads, writes):
        act(out_f32, in_ap, AF.Ln, reads, writes)
        act(out_f32, out_f32, AF.Exp, writes, writes, scale=-1.0)

    dbg_keys = []

    def dump_fm(dst, buf, nch, ntok):
        for kc in range(nch):
            for h0 in range(0, ntok, 1024):
                h1 = min(ntok, h0 + 1024)
                k = "dbg%d" % len(dbg_keys)
                dbg_keys.append(k)
                P.add("gpsimd", lambda e, kc=kc, h0=h0, h1=h1: e.dma_start(out=dst[kc, :, h0:h1], in_=buf[:, kc, h0:h1]),
                      [], [], dkey=k)

    def run_staggered(gens, width=64):
        gens = list(gens)
        active = []
        nxt = 0
        while nxt < len(gens) or active:
            for g in list(active):
                try:
                    next(g)
                except StopIteration:
                    active.remove(g)
            if nxt < len(gens) and len(active) < width:
                g = gens[nxt]
                nxt += 1
                try:
                    next(g)
                    active.append(g)
                except StopIteration:
                    pass

    eps_t = sb(top, "eps_t", [128, 1], F32)
    one_t = sb(top, "one_t", [128, 1], F32)
    memset("vector", eps_t[:], EPS, ["eps"])
    memset("vector", one_t[:], 1.0, ["one"])
    memset("vector", ones_bf[:], 1.0, ["ones_bf"])
    ident_bf = sb(top, "ident_bf", [128, 128], BF16)
    rowm = sb(top, "rowm", [128, 2], F32)
    memset("vector", rowm[:], 0.0, ["rowm"])
    memset("vector", rowm[0:64, 0:1], 1.0, ["rowm"])
    memset("vector", rowm[64:128, 1:2], 1.0, ["rowm"])
    dma("sync", ident[:], ident_d[:, :], [], ["ident"], "c_ident")
    vcopy(ident_bf[:], ident[:], ["ident"], ["ident_bf"])
    dma("sync", ng[:], ng_d.rearrange("l p k -> p l k"), [], ["ng"], "c_ng")
    dma("sync", adab[:], ada_b_d.rearrange("l p k -> p l k"), [], ["adab"], "c_adab")

    with ExitStack() as es:
        xin = [sb(es, "xin%d" % i, [128, D], F32) for i in range(3)]
        cv = sb(es, "cv", [128, 8, 2], F32)
        dma("sync", cv[:], cvec_d[:, :, :], [], ["cv"], "c_cv")
        act(silu_c[:], cv[:], AF.Silu, ["cv"], ["silu_c"])
        for t in range(NT):
            src = ctx_d[t * 128:(t + 1) * 128, :] if t < 2 else x_d[(t - 2) * 128:(t - 1) * 128, :]
            b = t % 3
            dma("sync", xin[b][:], src, [], [("xin", b)], "xin%d" % b)
            tb = tb_of_tile(t)
            for half in range(2):
                pi = ps_next()
                for q in range(4):
                    kc = half * 4 + q
                    tr(PS[pi][:, q * 128:(q + 1) * 128], xin[b][:, kc * 128:(kc + 1) * 128],
                       [("xin", b)], [("ps", pi)])
                dst = ST[:, half * 4:half * 4 + 4, t * 128:(t + 1) * 128]
                srcp = PS[pi][:, :].rearrange("p (a b) -> p a b", a=4)
                wk = [("ST", half * 4 + q, tb) for q in range(4)]
                if half == 0:
                    vcopy(dst, srcp, [("ps", pi)], wk)
                else:
                    act(dst, srcp, AF.Copy, [("ps", pi)], wk)
        P.barrier()

    def mod_block(es_bufs, l, blk, mod_t):
        buf = es_bufs[blk % 2]
        key = ("adaw", blk % 2)
        dma("gpsimd", buf[:], ada_w_d[l, :, blk * 512:(blk + 1) * 512].rearrange("(kc p) n -> p kc n", p=128),
            [], [key], "adaw%d" % (blk % 2))
        pi = ps_next()
        for q in range(4):
            fc = blk * 4 + q
            for kc in range(8):
                mm(PS[pi][:, q * 2:q * 2 + 2], buf[:, kc, q * 128:(q + 1) * 128], silu_c[:, kc, :],
                   kc == 0, kc == 7, [key, "silu_c"], [("ps", pi)])
        for j in range(2):
            o = mod_t[:, blk * 4:blk * 4 + 4, j]
            i0 = PS[pi][:, 0:8].rearrange("p (a b) -> p a b", b=2)[:, :, j]
            vtt(o, i0, adab[:, l, blk * 4:blk * 4 + 4], ALU.add, [("ps", pi), "adab"], [("mod", l % 2, blk)])

    def mod_finish(l, mod_t, which=(0, 1)):
        for n, sc0 in ((0, 8), (1, 32)):
            if n not in which:
                continue
            for j in range(2):
                P.add("vector", lambda e, n=n, sc0=sc0, j=j: e.scalar_tensor_tensor(
                    out=AB[:, n, :, j], in0=mod_t[:, sc0:sc0 + 8, j], scalar=1.0, in1=ng[:, l, n * 8:(n + 1) * 8],
                    op0=ALU.add, op1=ALU.mult),
                    [("mod", l % 2, b) for b in (sc0 // 4, sc0 // 4 + 1)] + ["ng"], [("AB", n)])

    def norm_bufs(es, tagp):
        sq = [sb(es, tagp + "sq%d" % i, [128, 8, 512], BF16) for i in range(2)]
        rs = [sb(es, tagp + "rs%d" % i, [128, 512], F32) for i in range(2)]
        tmp = [sb(es, tagp + "tmp%d" % i, [128, 512], F32) for i in range(3)]
        return (sq, rs, tmp, tagp)

    def norm_mod(nb, l, n, mod_t, tbs):
        sq, rs, tmp, tagp = nb
        sh0 = 0 if n == 0 else 24
        tcnt = [0]

        def gen(ci, tb):
            t0, n_ = TBS[tb]
            j = 1 if tb == 0 else 0
            sl = ci % 2
            act(sq[sl][:, 0:5, :n_], ST[:, 0:5, t0:t0 + n_], AF.Square, [("ST", kc, tb) for kc in range(5)],
                [(tagp + "sqa", sl)])
            vtt(sq[sl][:, 5:8, :n_], ST[:, 5:8, t0:t0 + n_], ST[:, 5:8, t0:t0 + n_], ALU.mult,
                [("ST", kc, tb) for kc in range(5, 8)], [(tagp + "sqb", sl)])
            yield
            pi = ps_next()
            for kc in range(8):
                mm(PS[pi][:, :n_], ones_bf[:], sq[sl][:, kc, :n_], kc == 0, kc == 7,
                   [(tagp + ("sqa" if kc < 5 else "sqb"), sl), "ones_bf"], [("ps", pi)])
            rsqrt_from(rs[sl][:, :n_], PS[pi][:, :n_], 1.0 / D, [("ps", pi), "eps"], [(tagp + "rs", sl)])
            yield
            for kc in range(8):
                ti = tcnt[0] % 3
                tcnt[0] += 1
                tt = tmp[ti]
                vstt(tt[:, :n_], ST[:, kc, t0:t0 + n_], AB[:, n, kc, j:j + 1], rs[sl][:, :n_], ALU.mult, ALU.mult,
                     [("ST", kc, tb), ("AB", n), (tagp + "rs", sl)], [(tagp + "tmp", ti)])
                act(Z[:, kc, t0:t0 + n_], tt[:, :n_], AF.Identity, [(tagp + "tmp", ti),
                    ("mod", l % 2, (sh0 + kc) // 4)], [("Z", kc, tb)],
                    bias=mod_t[:, sh0 + kc, j:j + 1], scale=1.0)
                if kc == 3:
                    yield

        run_staggered([gen(ci, tb) for ci, tb in enumerate(tbs)])

    def wload(buf, key, dkey, src):
        return dma("gpsimd", buf, src, [], [key], dkey)

    def proj_fm(wbuf, wkey, c0, m, tb, evac):
        t0, n_ = TBS[tb]
        pi = ps_next()
        for kc in range(8):
            mm(PS[pi][:m, :n_], wbuf[:, kc, c0:c0 + m], Z[:, kc, t0:t0 + n_], kc == 0, kc == 7,
               [wkey, ("Z", kc, tb)], [("ps", pi)])
        evac(pi, tb, t0, n_)

    def proj_tm(wbuf, wkey, c0, ncol, t, evac):
        tb = tb_of_tile(t)
        pi = ps_next()
        for kc in range(8):
            mm(PS[pi][:, :ncol], Z[:, kc, t * 128:(t + 1) * 128], wbuf[:, kc, c0:c0 + ncol], kc == 0, kc == 7,
               [wkey, ("Z", kc, tb)], [("ps", pi)])
        evac(pi, t)

    def mix_out(es, l, mod_t, MX, mxname, tbs):
        wb = [sb(es, "wmo%d" % i, [128, 8, 512], BF16) for i in range(2)]
        for hf in range(2):
            wload(wb[hf][:], ("wmo", hf), "wmo%d" % hf,
                  wmo_d[l, :, hf * 512:(hf + 1) * 512].rearrange("(kc p) n -> p kc n", p=128))
        for oc in range(8):
            hf, c0 = oc // 4, (oc % 4) * 128
            for tb in tbs:
                t0, n_ = TBS[tb]
                j = 1 if tb == 0 else 0
                pi = ps_next()
                for kc in range(8):
                    mm(PS[pi][:, :n_], wb[hf][:, kc, c0:c0 + 128], MX[:, kc, t0:t0 + n_], kc == 0, kc == 7,
                       [("wmo", hf), (mxname, kc, tb)], [("ps", pi)])
                vstt(ST[:, oc, t0:t0 + n_], PS[pi][:, :n_], mod_t[:, 16 + oc, j:j + 1], ST[:, oc, t0:t0 + n_],
                     ALU.mult, ALU.add, [("ps", pi), ("ST", oc, tb)] + [("mod", l % 2, b) for b in range(12)],
                     [("ST", oc, tb)])

    def ffn(es, l, mod_t, tbs, next_mod):
        G = 2
        NG = 22 // G
        w1 = [sb(es, "fw1_%d" % i, [128, 8, 4 * 128], BF16) for i in range(2)]
        w2 = [sb(es, "fw2_%d" % i, [128, G, D], BF16) for i in range(2)]
        actb = [sb(es, "fact%d" % i, [128, G, T], BF16) for i in range(2)]
        sg = [sb(es, "fsg%d" % i, [128, 512], F32) for i in range(2)]
        adabuf = None
        if next_mod is not None:
            adabuf = [sb(es, "adaw%d" % i, [128, 8, 512], BF16) for i in range(2)]

        def load(gi):
            b = gi % 2
            h0 = gi * G * 128
            wload(w1[b][:, :, 0:256], ("fw1g", b), "fw1g%d" % b,
                  fwi_d[l, :, h0:h0 + 256].rearrange("(kc p) n -> p kc n", p=128))
            wload(w1[b][:, :, 256:512], ("fw1u", b), "fw1u%d" % b,
                  fwi_d[l, :, FFN_H + h0:FFN_H + h0 + 256].rearrange("(kc p) n -> p kc n", p=128))
            wload(w2[b][:], ("fw2", b), "fw2_%d" % b,
                  fwo_d[l, h0:h0 + 256, :].rearrange("(g p) n -> p g n", p=128))

        load(0)
        sgi = 0
        for gi in range(NG):
            b = gi % 2
            if gi + 1 < NG:
                load(gi + 1)
            for jj in range(G):
                for tb in tbs:
                    t0, n_ = TBS[tb]
                    pg = ps_next()
                    for kc in range(8):
                        mm(PS[pg][:, :n_], w1[b][:, kc, jj * 128:(jj + 1) * 128], Z[:, kc, t0:t0 + n_], kc == 0,
                           kc == 7, [("fw1g", b), ("Z", kc, tb)], [("ps", pg)])
                    pu = ps_next()
                    for kc in range(8):
                        mm(PS[pu][:, :n_], w1[b][:, kc, 256 + jj * 128:256 + (jj + 1) * 128], Z[:, kc, t0:t0 + n_],
                           kc == 0, kc == 7, [("fw1u", b), ("Z", kc, tb)], [("ps", pu)])
                    s = sg[sgi % 2]
                    skey = ("fsg", sgi % 2)
                    sgi += 1
                    act(s[:, :n_], PS[pg][:, :n_], AF.Silu, [("ps", pg)], [skey])
                    vtt(actb[b][:, jj, t0:t0 + n_], PS[pu][:, :n_], s[:, :n_], ALU.mult, [("ps", pu), skey],
                        [("fact", b, jj, tb)])
            for tb in tbs:
                for oc in range(8):
                    t0, n_ = TBS[tb]
                    j = 1 if tb == 0 else 0
                    pi = ps_next()
                    for jj in range(G):
                        mm(PS[pi][:, :n_], w2[b][:, jj, oc * 128:(oc + 1) * 128], actb[b][:, jj, t0:t0 + n_],
                           jj == 0, jj == G - 1, [("fw2", b), ("fact", b, jj, tb)], [("ps", pi)])
                    vstt(ST[:, oc, t0:t0 + n_], PS[pi][:, :n_], mod_t[:, 40 + oc, j:j + 1], ST[:, oc, t0:t0 + n_],
                         ALU.mult, ALU.add, [("ps", pi), ("ST", oc, tb)] + [("mod", l % 2, bb) for bb in range(12)],
                         [("ST", oc, tb)])
            if next_mod is not None:
                nl, nmod = next_mod
                mod_block(adabuf, nl, gi, nmod)
                if gi == NG - 1:
                    mod_block(adabuf, nl, 11, nmod)

    def sg_mixer(es0, l, MX, tbs, tiles):
        i = l // 2
        with ExitStack() as es:
            VN = sb(es, "VN", [128, NT, 512], BF16)
            wu = [sb(es, "wu%d" % k, [128, 8, 512], BF16) for k in range(2)]
            vng = sb(es, "vng", [128, 512], F32)
            wsT = sb(es, "wsT", [128, 4, 128], BF16)
            bsr = sb(es, "bsr", [1, 512], BF16)
            gsv = [sb(es, "gsv%d" % k, [128, 512], F32) for k in range(4)]
            junk = [sb(es, "sgjunk%d" % k, [128, 512], F32) for k in range(2)]
            ssq = [sb(es, "ssq%d" % k, [128, 4], F32) for k in range(4)]
            dma("sync", vng[:], vng_d[i, :, :], [], ["vng"], "c_vng")
            wload(wsT[:], "wsT", "c_wsT", wst_d[i, :, :, :])
            wload(bsr[:], "bsr", "c_bsr", sbs_d[i, :, :])
            wload(wu[0][:], ("wu", 0), "wu0", evw_d[i, :, 1568:2080].rearrange("(kc p) n -> p kc n", p=128))
            wload(wu[1][:], ("wu", 1), "wu1", evw_d[i, :, 2080:2592].rearrange("(kc p) n -> p kc n", p=128))
            for g in range(4):
                for tb in tbs:
                    def ev(pi, tb, t0, n_, g=g):
                        act(MX[:, 4 + g, t0:t0 + n_], PS[pi][:, :n_], AF.Gelu, [("ps", pi)], [("MX", 4 + g, tb)])
                    proj_fm(wu[0], ("wu", 0), g * 128, 128, tb, ev)
            def sv_gen(n, t):
                tb = tb_of_tile(t)
                sl = n % 4
                gs = gsv[sl]
                gk = ("gsv", sl)
                sk = ("ssq", sl)
                pi = ps_next()
                for kc in range(8):
                    mm(PS[pi][:, :512], Z[:, kc, t * 128:(t + 1) * 128], wu[1][:, kc, 0:512], kc == 0, kc == 7,
                       [("wu", 1), ("Z", kc, tb)], [("ps", pi)])
                act(gs[:], PS[pi][:, :], AF.Gelu, [("ps", pi)], [gk])
                yield
                jk = junk[n % 2]
                vtt(jk[:], gs[:], gs[:], ALU.mult, [gk], [("sgjunk", n % 2)])
                P.add("vector", lambda e: e.tensor_reduce(out=ssq[sl][:, :], in_=jk[:, :].rearrange("p (a b) -> p a b", a=4),
                                                          axis=mybir.AxisListType.X, op=ALU.add),
                      [("sgjunk", n % 2)], [sk])
                rsqrt_from(ssq[sl][:, :], ssq[sl][:, :], 1.0 / 128, [sk, "eps"], [sk])
                yield
                for g in range(4):
                    vstt(VN[:, t, g * 128:(g + 1) * 128], gs[:, g * 128:(g + 1) * 128], ssq[sl][:, g:g + 1],
                         vng[:, g * 128:(g + 1) * 128], ALU.mult, ALU.mult, [gk, sk, "vng"], [("VN", t)])
                yield
                pi = ps_next()
                for g in range(4):
                    mm(PS[pi][:, g * 128:(g + 1) * 128], VN[:, t, g * 128:(g + 1) * 128], wsT[:, g, :], True, False,
                       [("VN", t), "wsT"], [("ps", pi)])
                    mm(PS[pi][:, g * 128:(g + 1) * 128], ones_bf[0:1, :], bsr[0:1, g * 128:(g + 1) * 128], False,
                       True, ["ones_bf", "bsr"], [("ps", pi)])
                yield
                dst = MX[:, 4:8, t * 128:(t + 1) * 128]
                vtt(dst, PS[pi][:, :].rearrange("p (a b) -> p a b", a=4), dst, ALU.mult,
                    [("ps", pi)] + [("MX", 4 + g, tb) for g in range(4)], [("MX", 4 + g, tb) for g in range(4)])

            run_staggered([sv_gen(n, t) for n, t in enumerate(tiles)])
            P.barrier()

    def gla_mixer(es0, l, MX, tbs, tiles, tri, A_T):
        i = l // 2
        gon = sb(es0, "gon", [128, 1], F32)
        wab = sb(es0, "wab", [33, 2, 256], BF16)
        dma("sync", gon[:], gon_d[i, :, :], [], ["gon"], "c_gon")
        wload(wab[:], "wab", "c_wab", wab_d[i, :, :, :])
        ctx_tiles = [t for t in tiles if t < 2]
        lat_tiles = [t for t in tiles if t >= 2]
        for p in range(2):
            with ExitStack() as es:
                QT = sb(es, "QT", [128, T], BF16)
                KT = sb(es, "KT", [128, T], BF16)
                Ktok = sb(es, "Ktok", [128, NT, 128], BF16)
                Vtok = sb(es, "Vtok", [128, NT, 256], BF16)
                with ExitStack() as es2:
                    wq = sb(es2, "wq", [128, 8, 128], BF16)
                    wk = sb(es2, "wk", [128, 8, 128], BF16)
                    wv = sb(es2, "wv", [128, 8, 256], BF16)
                    wg = sb(es2, "wg", [128, 8, 256], BF16)
                    wa = sb(es2, "wa", [128, 8, 32], BF16)
                    rr = lambda c0, n: evw_d[i, :, c0:c0 + n].rearrange("(kc p) n -> p kc n", p=128)
                    wload(wq[:], "wq", "wq", rr(p * 128, 128))
                    wload(wk[:], "wk", "wk", rr(256 + p * 128, 128))
                    wload(wv[:], "wv", "wv", rr(512 + p * 256, 256))
                    wload(wg[:], "wg", "wg", rr(1024 + p * 256, 256))
                    if p == 0:
                        wload(wa[:], "wa", "wa", rr(1536, 32))
                    for tb in tbs:
                        proj_fm(wq, "wq", 0, 128, tb, lambda pi, tb, t0, n_: act(
                            QT[:, t0:t0 + n_], PS[pi][:, :n_], AF.Copy, [("ps", pi)], [("QT", tb)], scale=0.125))
                        proj_fm(wk, "wk", 0, 128, tb, lambda pi, tb, t0, n_: vcopy(
                            KT[:, t0:t0 + n_], PS[pi][:, :n_], [("ps", pi)], [("KT", tb)]))
                        for hh in range(2):
                            proj_fm(wg, "wg", hh * 128, 128, tb, lambda pi, tb, t0, n_, hh=hh: act(
                                MX[:, 2 * p + hh, t0:t0 + n_], PS[pi][:, :n_], AF.Silu, [("ps", pi)],
                                [("MX", 2 * p + hh, tb)]))
                        if p == 0:
                            proj_fm(wa, "wa", 0, 32, tb, lambda pi, tb, t0, n_: vcopy(
                                A_T[0:32, t0:t0 + n_], PS[pi][:32, :n_], [("ps", pi)], [("A_T", tb)]))
                    for t in tiles:
                        proj_tm(wk, "wk", 0, 128, t, lambda pi, t: vcopy(
                            Ktok[:, t, :], PS[pi][:, :128], [("ps", pi)], [("Ktok", t)]))
                        proj_tm(wv, "wv", 0, 256, t, lambda pi, t: act(
                            Vtok[:, t, :], PS[pi][:, :256], AF.Copy, [("ps", pi)], [("Vtok", t)]))
                    P.barrier()
                with ExitStack() as es2:
                    OF = sb(es2, "OF", [128, 2, T], BF16)
                    es_sc = ExitStack()
                    es2_outer = es2
                    es2 = es_sc
                    Sf = [sb(es2, "Sf%d" % k, [128, 128], F32) for k in range(2)]
                    Sb = [sb(es2, "Sb%d" % k, [128, 128], BF16) for k in range(8)]
                    def slots(name, n, shape, dt):
                        return [sb(es2, "%s%d" % (name, k), shape, dt) for k in range(n)]
                    Gt = slots("Gt", 2, [128, 128], BF16)
                    Et = slots("Et", 2, [128, 128], F32)
                    EB = slots("EB", 6, [128, 128], F32)
                    ENB = slots("ENB", 2, [128, 128], F32)
                    ED = slots("ED", 2, [128, 128], F32)
                    QE = [slots("QE%d_" % hh, 6, [128, 128], BF16) for hh in range(2)]
                    KE = slots("KE", 2, [128, 128], BF16)
                    KL = [slots("KL%d_" % c, 2, [128, 128], BF16) for c in range(2)]
                    ATs = [slots("ATs%d_" % hh, 2, [128, 128], BF16) for hh in range(2)]

                    def pick(lst, n):
                        i = n % len(lst)
                        return lst[i], i

                    for d in range(2):
                        order = (ctx_tiles + lat_tiles) if d == 0 else (ctx_tiles[::-1] + lat_tiles[::-1])
                        tri_incl = tri[:, 0 + 3 * d, :]
                        tri_strict = tri[:, 1 + 3 * d, :]
                        mask = tri[:, 2 + 3 * d, :]
                        memset("vector", Sf[0][:], 0.0, [("Sf", 0)])
                        memset("vector", Sb[0][:], 0.0, [("Sb", 0)])
                        sfi = [0]
                        sbi = [0]
                        chunks = (0, 1) if d == 0 else (1, 0)

                        def tile_gen(n, t):
                            tb = tb_of_tile(t)
                            tok = slice(t * 128, (t + 1) * 128)
                            et, eti = pick(Et, n)
                            gt, gti = pick(Gt, n)
                            eb, ebi = pick(EB, n)
                            enb, enbi = pick(ENB, n)
                            ed, edi = pick(ED, n)
                            ke, kei = pick(KE, n)
                            qe = [pick(QE[hh], n) for hh in range(2)]
                            kl = [pick(KL[c], n) for c in range(2)]
                            ats = [pick(ATs[hh], n) for hh in range(2)]
                            pp = 0
                            mm(PS[pp][:, 0:128], A_T[0:33, tok], wab[0:33, p, d * 128:(d + 1) * 128], True, True,
                               [("A_T", tb), "A_T1", "wab"], [("ps", pp)])
                            act(et[:], PS[pp][:, 0:128], AF.Exp, [("ps", pp)], [("Et", eti)], scale=-1.0)
                            act(gt[:], et[:], AF.Ln, [("Et", eti), "one"], [("Gt", gti)], bias=one_t[:, :], scale=1.0)
                            yield
                            pb = 1
                            mm(PS[pb][:, 0:128], gt[:], tri_incl, True, True, [("Gt", gti), "tri"], [("ps", pb)])
                            mm(PS[pb][:, 128:256], tri_strict, gt[:], True, True, [("Gt", gti), "tri"], [("ps", pb)])
                            act(eb[:], PS[pb][:, 0:128], AF.Exp, [("ps", pb)], [("EB", ebi)])
                            act(enb[:], PS[pb][:, 0:128], AF.Exp, [("ps", pb)], [("ENB", enbi)], scale=-1.0)
                            act(ed[:], PS[pb][:, 128:256], AF.Exp, [("ps", pb)], [("ED", edi)])
                            yield
                            for hh in range(2):
                                vstt(qe[hh][0][:], QT[:, tok], rowm[:, hh:hh + 1], eb[:], ALU.mult, ALU.mult,
                                     [("QT", tb), ("EB", ebi), "rowm"], [("QE", hh, qe[hh][1])])
                            vtt(ke[:], KT[:, tok], enb[:], ALU.mult, [("KT", tb), ("ENB", enbi)], [("KE", kei)])
                            for c in range(2):
                                vstt(kl[c][0][:], Ktok[:, t, :], rowm[:, c:c + 1], ed[:], ALU.mult, ALU.mult,
                                     [("Ktok", t), ("ED", edi), "rowm"], [("KL", c, kl[c][1])])
                            yield
                            pS = 5 + n % 3
                            for c in range(2):
                                for hh in range(2):
                                    hs = slice(hh * 64, (hh + 1) * 64)
                                    mm(PS[pS][hs, c * 128:(c + 1) * 128], kl[c][0][:, hs],
                                       Vtok[:, t, hh * 128:(hh + 1) * 128], True, True,
                                       [("KL", c, kl[c][1]), ("Vtok", t)], [("ps", pS)])
                            pa = 2
                            for hh in range(2):
                                mm(PS[pa][:, hh * 128:(hh + 1) * 128], ke[:, :], qe[hh][0][:, :], True, True,
                                   [("KE", kei), ("QE", hh, qe[hh][1])], [("ps", pa)])
                            for hh in range(2):
                                vtt(ats[hh][0][:], PS[pa][:, hh * 128:(hh + 1) * 128], mask, ALU.mult, [("ps", pa), "tri"],
                                    [("ATs", hh, ats[hh][1])])
                            yield
                            pq = 3
                            for hh in range(2):
                                hc = slice(hh * 128, (hh + 1) * 128)
                                if d == 1:
                                    mm(PS[pq][:, hc], ident_bf[:], OF[:, hh, tok], True, False,
                                       ["ident_bf", ("OF", hh, t)], [("ps", pq)])
                                mm(PS[pq][:, hc], Vtok[:, t, hh * 128:(hh + 1) * 128], ats[hh][0][:], d == 0, True,
                                   [("Vtok", t), ("ATs", hh, ats[hh][1])], [("ps", pq)])
                            act(OF[:, :, tok], PS[pq][:, 0:256].rearrange("p (a b) -> p a b", a=2), AF.Copy,
                                [("ps", pq)], [("OF", 0, t), ("OF", 1, t)])
                            yield
                            snaps = []
                            for c in chunks:
                                last_col = (c * 64 + 63) if d == 0 else (c * 64)
                                a, b2 = sfi[0], 1 - sfi[0]
                                sfi[0] = b2
                                jp = sbi[0]
                                jn = (jp + 1) % len(Sb)
                                sbi[0] = jn
                                snaps.append((c, jp))
                                vstt(Sf[b2][:], Sf[a][:], eb[:, last_col:last_col + 1],
                                     PS[pS][:, c * 128:(c + 1) * 128], ALU.mult, ALU.add,
                                     [("Sf", a), ("EB", ebi), ("ps", pS)], [("Sf", b2)])
                                act(Sb[jn][:], Sf[b2][:], AF.Copy, [("Sf", b2)], [("Sb", jn)])
                            yield
                            pI = 4
                            mm(PS[pI][:, 0:256].rearrange("p (a b) -> p a b", a=2), ident_bf[:], OF[:, :, tok], True, False,
                               ["ident_bf", ("OF", 0, t), ("OF", 1, t)], [("ps", pI)])
                            for ci_, (c, j) in enumerate(snaps):
                                cs = slice(c * 64, (c + 1) * 64)
                                for hh in range(2):
                                    mm(PS[pI][:, hh * 128 + c * 64:hh * 128 + (c + 1) * 64], Sb[j][:, :],
                                       qe[hh][0][:, cs], False, ci_ == 1 and hh == 1,
                                       [("Sb", j), ("QE", hh, qe[hh][1])], [("ps", pI)])
                            act(OF[:, :, tok], PS[pI][:, 0:256].rearrange("p (a b) -> p a b", a=2), AF.Copy,
                                [("ps", pI)], [("OF", 0, t), ("OF", 1, t)])

                        run_staggered([tile_gen(n, t) for n, t in enumerate(order)])
                    P.barrier()
                    es_sc.close()
                    es2 = es2_outer
                    osum = [sb(es2, "osum%d" % k, [128, 512], F32) for k in range(2)]
                    osq = [sb(es2, "osq%d" % k, [128, 512], BF16) for k in range(2)]
                    ors = [sb(es2, "ors%d" % k, [128, 512], F32) for k in range(2)]
                    def fin_gen(fi, hh, tb):
                        ch = 2 * p + hh
                        t0, n_ = TBS[tb]
                        k = fi % 2
                        okeys = [("OF", hh, t) for t in tiles if tb_of_tile(t) == tb]
                        act(osq[k][:, :n_], OF[:, hh, t0:t0 + n_], AF.Square, okeys, [("osq", k)])
                        yield
                        pn = ps_next()
                        mm(PS[pn][:, :n_], ones_bf[:], osq[k][:, :n_], True, True, ["ones_bf", ("osq", k)],
                           [("ps", pn)])
                        yield
                        rsqrt_from(ors[k][:, :n_], PS[pn][:, :n_], 1.0 / 128, [("ps", pn), "eps"], [("ors", k)])
                        yield
                        vstt(osum[k][:, :n_], OF[:, hh, t0:t0 + n_], gon[:, 0:1], ors[k][:, :n_], ALU.mult, ALU.mult,
                             okeys + ["gon", ("ors", k)], [("osum", k)])
                        vtt(MX[:, ch, t0:t0 + n_], osum[k][:, :n_], MX[:, ch, t0:t0 + n_], ALU.mult,
                            [("osum", k), ("MX", ch, tb)], [("MX", ch, tb)])

                    fgens = []
                    for hh in range(2):
                        for tb in tbs:
                            fgens.append(fin_gen(len(fgens), hh, tb))
                    run_staggered(fgens, width=2)
                    P.barrier()

    def chk(name):
        if stop == name:
            raise _Stop()

    def main_layers():
        with ExitStack() as es:
            adabuf = [sb(es, "adaw%d" % i, [128, 8, 512], BF16) for i in range(2)]
            for blk in range(4):
                mod_block(adabuf, 0, blk, MOD[0])
            mod_finish(0, MOD[0], which=(0,))
            norm_mod(norm_bufs(es, "nm"), 0, 0, MOD[0], [0, 1, 2, 3, 4])
            for blk in range(4, 12):
                mod_block(adabuf, 0, blk, MOD[0])
            mod_finish(0, MOD[0], which=(1,))
            P.barrier()
        chk("mod0")
        for l in range(n_layers):
            mod_t = MOD[l % 2]
            need_ctx = l < DEPTH - 1
            tbs_all = [0, 1, 2, 3, 4]
            tbs_res = tbs_all if need_ctx else [1, 2, 3, 4]
            tiles_all = list(range(NT))
            chk("norm0_%d" % l)
            if l % 2 == 0:
                with ExitStack() as es:
                    MX = sb(es, "MX", [128, 8, T], BF16)
                    tri = sb(es, "tri", [128, 6, 128], BF16)
                    A_T = sb(es, "A_T", [33, T], BF16)
                    wload(tri[:], "tri", "c_tri", tri_d[:, :, :])
                    memset("vector", A_T[32:33, :], 1.0, ["A_T1"])
                    sg_mixer(es, l, MX, tbs_all, tiles_all)
                    chk("sg_%d" % l)
                    gla_mixer(es, l, MX, tbs_all, tiles_all, tri, A_T)
                    chk("gla_%d" % l)
                    if dbg and ("mx%d" % l) in dbg_d:
                        P.barrier()
                        dump_fm(dbg_d["mx%d" % l], MX, 8, T)
                        P.barrier()
                    with ExitStack() as es2:
                        mix_out(es2, l, mod_t, MX, "MX", tbs_res)
                        P.barrier()
            else:
                odd_mixer(l, mod_t, tbs_all, tbs_res, need_ctx)
            chk("mix_%d" % l)
            with ExitStack() as es:
                nb = norm_bufs(es, "nm")
                norm_mod(nb, l, 1, mod_t, tbs_res)
                chk("norm1_%d" % l)
                nm = None
                if l + 1 < n_layers:
                    nm = (l + 1, MOD[(l + 1) % 2])
                ffn(es, l, mod_t, tbs_res, nm)
                if nm is not None:
                    mod_finish(l + 1, MOD[(l + 1) % 2])
                    norm_mod(nb, l + 1, 0, MOD[(l + 1) % 2], tbs_all)
                P.barrier()
            chk("ffn_%d" % l)

    h_dbg = [True]

    def odd_mixer(l, mod_t, tbs_all, tbs_res, need_ctx):
        i = l // 2
        MX = Z
        SC = 192.0 ** -0.5
        q_tbs = tbs_all if need_ctx else [1, 2, 3, 4]
        with ExitStack() as es:
            QAN = sb(es, "QAN", [128, 3, T], BF16)
            KVAN = sb(es, "KVAN", [128, 2, T], BF16)
            KPE = sb(es, "KPE", [64, T], BF16)
            qag = sb(es, "qag", [128, 3], F32)
            kvg = sb(es, "kvg", [128, 2], F32)
            qkn = sb(es, "qkn", [128, 4], F32)
            dma("sync", qag[:], qag_d[i, :, :], [], ["qag"], "c_qag")
            dma("sync", kvg[:], kvg_d[i, :, :], [], ["kvg"], "c_kvg")
            dma("sync", qkn[:], qkn_d[i, :, :], [], ["qkn"], "c_qkn")
            with ExitStack() as esA:
                FT = sb(esA, "FT", [128, 2, T], BF16)
                with ExitStack() as es1:
                    odw = sb(es1, "odw", [128, 8, 960], BF16)
                    sqt = [sb(es1, "sqt%d" % k, [128, 3, 512], BF16) for k in range(2)]
                    rsd = [sb(es1, "rsd%d" % k, [128, 512], F32) for k in range(2)]
                    for gi, (c0, c1) in enumerate(((0, 256), (256, 640), (640, 896), (896, 960))):
                        wload(odw[:, :, c0:c1], ("odw", gi), "odw%d" % gi,
                              odw_d[i, :, c0:c1].rearrange("(kc p) n -> p kc n", p=128))
                    def ft_gen(tb):
                        for j in range(2):
                            proj_fm(odw, ("odw", 0), j * 128, 128, tb, lambda pi, tb, t0, n_, j=j: vcopy(
                                FT[:, j, t0:t0 + n_], PS[pi][:, :n_], [("ps", pi)], [("FT", j, tb)]))
                        return
                        yield

                    def lat_gen(ci, tb, nm, nch, c0, gkey, gt, dstb, wk):
                        t0, n_ = TBS[tb]
                        sl = ci % 2
                        held = []
                        for c in range(nch):
                            pi = ps_next(hold=True)
                            held.append(pi)
                            for kc in range(8):
                                mm(PS[pi][:, :n_], odw[:, kc, c0 + c * 128:c0 + (c + 1) * 128], Z[:, kc, t0:t0 + n_],
                                   kc == 0, kc == 7, [("odw", wk), ("Z", kc, tb)], [("ps", pi)])
                            act(sqt[sl][:, c, :n_], PS[pi][:, :n_], AF.Square, [("ps", pi)], [("sqt", sl, c)])
                        yield
                        pss = ps_next()
                        for c in range(nch):
                            mm(PS[pss][:, :n_], ones_bf[:], sqt[sl][:, c, :n_], c == 0, c == nch - 1,
                               [("sqt", sl, c), "ones_bf"], [("ps", pss)])
                        rsqrt_from(rsd[sl][:, :n_], PS[pss][:, :n_], 1.0 / (128 * nch), [("ps", pss), "eps"],
                                   [("rsd", sl)])
                        yield
                        for c in range(nch):
                            vstt(dstb[:, c, t0:t0 + n_], PS[held[c]][:, :n_], gt[:, c:c + 1], rsd[sl][:, :n_], ALU.mult,
                                 ALU.mult, [("ps", held[c]), gkey, ("rsd", sl)], [(nm + "n", c, tb)])
                            ps_release(held[c])

                    def kpe_gen(tb):
                        def evk(pi, tb, t0, n_):
                            vcopy(KPE[:, t0:t0 + n_], PS[pi][:64, :n_], [("ps", pi)], [("KPE", tb)])
                        proj_fm(odw, ("odw", 3), 896, 64, tb, evk)
                        return
                        yield

                    gens = []
                    ci = 0
                    for tb in tbs_all:
                        gens.append(ft_gen(tb))
                        gens.append(lat_gen(ci, tb, "qa", 3, 256, "qag", qag, QAN, 1))
                        ci += 1
                        gens.append(lat_gen(ci, tb, "kva", 2, 640, "kvg", kvg, KVAN, 2))
                        ci += 1
                        gens.append(kpe_gen(tb))
                    run_staggered(gens, width=O1W)
                    P.barrier()
                with ExitStack() as es2:
                    c64 = sb(es2, "c64", [128, 256], BF16)
                    Y = sb(es2, "Y", [128, NT, 512], BF16)
                    dfb = [sb(es2, "dfb%d" % k, [128, 2, SEQ], BF16) for k in range(2)]
                    wload(c64[:], "c64", "c_c64", c64_d[:, :])
                    f_tiles = list(range(NT)) if need_ctx else list(range(2, NT))
                    for t in f_tiles:
                        tb = tb_of_tile(t)
                        pi = ps_next()
                        for j in range(2):
                            mm(PS[pi][:, j * 256:(j + 1) * 256], FT[:, j, t * 128:(t + 1) * 128], c64[:], True, True,
                               [("FT", j, tb), "c64"], [("ps", pi)])
                        sc = (64.0 * (CTX if t < 2 else SEQ)) ** -0.5
                        act(Y[:, t, :], PS[pi][:, :], AF.Copy, [("ps", pi)], [("Y", t)], scale=sc)
                    for lt in range(16):
                        b = lt % 2
                        for cs in range(2):
                            dma("sync", dfb[b][:, cs, :], dftL_d[lt, :, cs, :], [], [("dfb", b, cs)], "dfb%d_%d" % (b, cs))
                        for j in range(2):
                            for lb in range(4):
                                pi = j * 4 + lb
                                for cs in range(2):
                                    mm(PS[pi][:, :], Y[:, 2 + lt, j * 256 + cs * 128:j * 256 + (cs + 1) * 128],
                                       dfb[b][:, cs, lb * 512:(lb + 1) * 512], lt == 0 and cs == 0,
                                       lt == 15 and cs == 1, [("Y", 2 + lt), ("dfb", b, cs)], [("ps", pi)])
                    for j in range(2):
                        for lb in range(4):
                            pi = j * 4 + lb
                            dst = MX[:, j, 256 + lb * 512:256 + (lb + 1) * 512]
                            if lb % 2 == 0:
                                vcopy(dst, PS[pi][:, :], [("ps", pi)], [("Z", j, 1 + lb)])
                            else:
                                act(dst, PS[pi][:, :], AF.Copy, [("ps", pi)], [("Z", j, 1 + lb)])
                    if need_ctx:
                        for lt in range(2):
                            for cs in range(2):
                                dma("sync", dfb[lt][:, cs, 0:CTX], dftC_d[lt, :, cs, :], [], [("dfb", lt, cs)], "dfb%d_%d" % (lt, cs))
                        for j in range(2):
                            pi = ps_next()
                            for lt in range(2):
                                for cs in range(2):
                                    mm(PS[pi][:, :CTX], Y[:, lt, j * 256 + cs * 128:j * 256 + (cs + 1) * 128],
                                       dfb[lt][:, cs, 0:CTX], lt == 0 and cs == 0, lt == 1 and cs == 1,
                                       [("Y", lt), ("dfb", lt, cs)], [("ps", pi)])
                            vcopy(MX[:, j, 0:CTX], PS[pi][:, :CTX], [("ps", pi)], [("Z", j, 0)])
                    P.barrier()
            with ExitStack() as es3:
                rope = sb(es3, "rope", [64, 2, SEQ], BF16)
                rmat = sb(es3, "rmat", [64, 64], BF16)
                wload(rope[:], "rope", "c_rope", rope_d[:, :, :])
                wload(rmat[:], "rmat", "c_rmat", rmat_d[:, :])
                wq = [sb(es3, "wuq%d" % k, [128, 3, 192], BF16) for k in range(2)]
                wkv = [sb(es3, "wukv%d" % k, [128, 2, 256], BF16) for k in range(2)]
                QN = sb(es3, "QN", [128, T], BF16)
                QR = sb(es3, "QR", [128, T], BF16)
                KN = [sb(es3, "KN%d" % k, [128, T], BF16) for k in range(2)]
                KR = [sb(es3, "KR%d" % k, [128, T], BF16) for k in range(2)]
                Vt = [sb(es3, "Vt%d" % k, [128, NT, 128], BF16) for k in range(2)]
                sqn = [sb(es3, "sqn%d" % k, [128, 512], BF16) for k in range(2)]
                sqr = [sb(es3, "sqr%d" % k, [64, 512], BF16) for k in range(2)]
                rst = [sb(es3, "rst%d" % k, [128, 512], F32) for k in range(2)]
                tq = [sb(es3, "tq%d" % k, [64, 512], F32) for k in range(2)]
                tqb = [sb(es3, "tqb%d" % k, [64, 512], BF16) for k in range(2)]
                Pt = [sb(es3, "Pt%d" % k, [128, 512], BF16) for k in range(3)]
                rec = sb(es3, "rec", [128, 512], F32)
                if h_dbg[0]:
                    print("attention phase arena use:", arena["cur"] - a_base)
                    h_dbg[0] = False
                memset("vector", QR[64:128, :], 0.0, [("QR", tb) for tb in range(5)])
                for k in range(2):
                    memset("vector", KR[k][64:128, :], 0.0, [("KR", k, tb) for tb in range(5)])

                def load_w(h):
                    b = h % 2
                    wload(wq[b][:], ("wuq", b), "wuq%d" % b,
                          wuq_d[i, :, h * 192:(h + 1) * 192].rearrange("(kc p) n -> p kc n", p=128))
                    wload(wkv[b][:], ("wukv", b), "wukv%d" % b,
                          wukv_d[i, :, h * 256:(h + 1) * 256].rearrange("(kc p) n -> p kc n", p=128))

                def qk_chain(cn, h, tb, is_q, light=True):
                    b = h % 2
                    t0, n_ = TBS[tb]
                    sl = cn % 2
                    gcol = 0 if is_q else 2
                    if is_q:
                        w_, wkey, nkc, src, skey = wq[b], ("wuq", b), 3, QAN, "qan"
                        dN, dR, nkey, rkey = QN, QR, ("QN", tb), ("QR", tb)
                    else:
                        w_, wkey, nkc, src, skey = wkv[b], ("wukv", b), 2, KVAN, "kvan"
                        dN, dR, nkey, rkey = KN[b], KR[b], ("KN", b, tb), ("KR", b, tb)
                    pn_ = bg_alloc()
                    for kc in range(nkc):
                        mm(PS[pn_][:, :n_], w_[:, kc, 0:128], src[:, kc, t0:t0 + n_], kc == 0, kc == nkc - 1,
                           [wkey, (skey, kc, tb)], [("ps", pn_)])
                    pr_ = None
                    if is_q:
                        pr_ = bg_alloc()
                        for kc in range(3):
                            mm(PS[pr_][:64, :n_], w_[:, kc, 128:192], src[:, kc, t0:t0 + n_], kc == 0, kc == 2,
                               [wkey, (skey, kc, tb)], [("ps", pr_)])
                    yield
                    act(sqn[sl][:, :n_], PS[pn_][:, :n_], AF.Square, [("ps", pn_)], [("sqn", sl)])
                    if is_q:
                        act(sqr[sl][:, :n_], PS[pr_][:64, :n_], AF.Square, [("ps", pr_)], [("sqr", sl)])
                        if light:
                            act(tq[sl][:, :n_], PS[pr_][:64, :n_], AF.Copy, [("ps", pr_)], [("tq", sl)])
                            bg_free(pr_)
                    else:
                        act(sqr[sl][:, :n_], KPE[:, t0:t0 + n_], AF.Square, [("KPE", tb)], [("sqr", sl)])
                    yield
                    pss = bg_alloc()
                    mm(PS[pss][:, :n_], ones_bf[:], sqn[sl][:, :n_], True, False, [("sqn", sl), "ones_bf"],
                       [("ps", pss)])
                    mm(PS[pss][:, :n_], ones_bf[0:64, :], sqr[sl][:, :n_], False, True, [("sqr", sl), "ones_bf"],
                       [("ps", pss)])
                    yield
                    rsqrt_from(rst[sl][:, :n_], PS[pss][:, :n_], 1.0 / 192, [("ps", pss), "eps"], [("rst", sl)])
                    bg_free(pss)
                    yield
                    vstt(dN[:, t0:t0 + n_], PS[pn_][:, :n_], qkn[:, gcol:gcol + 1], rst[sl][:, :n_], ALU.mult,
                         ALU.mult, [("ps", pn_), "qkn", ("rst", sl)], [nkey])
                    bg_free(pn_)
                    if is_q:
                        if light:
                            vstt(tq[sl][:, :n_], tq[sl][:, :n_], qkn[0:64, 1:2], rst[sl][0:64, :n_], ALU.mult,
                                 ALU.mult, [("tq", sl), "qkn", ("rst", sl)], [("tq", sl)])
                        else:
                            vstt(tq[sl][:, :n_], PS[pr_][:64, :n_], qkn[0:64, 1:2], rst[sl][0:64, :n_], ALU.mult,
                                 ALU.mult, [("ps", pr_), "qkn", ("rst", sl)], [("tq", sl)])
                            bg_free(pr_)
                    else:
                        vstt(tq[sl][:, :n_], KPE[:, t0:t0 + n_], qkn[0:64, 3:4], rst[sl][0:64, :n_], ALU.mult,
                             ALU.mult, [("KPE", tb), "qkn", ("rst", sl)], [("tq", sl)])
                    dst = dR[0:64, t0:t0 + n_]
                    if tb == 0:
                        vcopy(dst, tq[sl][:, :n_], [("tq", sl)], [rkey])
                        return
                    vcopy(tqb[sl][:, :n_], tq[sl][:, :n_], [("tq", sl)], [("tqb", sl)])
                    yield
                    l0 = t0 - CTX
                    pr = bg_alloc()
                    mm(PS[pr][:64, :n_], rmat[:, :], tqb[sl][:, :n_], True, True, ["rmat", ("tqb", sl)], [("ps", pr)])
                    yield
                    vtt(tq[sl][:, :n_], tq[sl][:, :n_], rope[:, 0, l0:l0 + n_], ALU.mult, [("tq", sl), "rope"],
                        [("tq", sl)])
                    vtt(PS[pr][:64, :n_], PS[pr][:64, :n_], rope[:, 1, l0:l0 + n_], ALU.mult, [("ps", pr), "rope"],
                        [("ps", pr)])
                    vtt(dst, tq[sl][:, :n_], PS[pr][:64, :n_], ALU.add, [("tq", sl), ("ps", pr)], [rkey])
                    bg_free(pr)

                def v_chain(h, t):
                    b = h % 2
                    tb = tb_of_tile(t)
                    pv = bg_alloc()
                    for kc in range(2):
                        mm(PS[pv][:, 0:128], KVAN[:, kc, t * 128:(t + 1) * 128], wkv[b][:, kc, 128:256], kc == 0,
                           kc == 1, [("wukv", b), ("kvan", kc, tb)], [("ps", pv)])
                    yield
                    vcopy(Vt[b][:, t, :], PS[pv][:, 0:128], [("ps", pv)], [("Vt", b, t)])
                    bg_free(pv)

                bgp = {"banks": list(range(8)), "held": set(), "rr": 0}

                def bg_alloc():
                    nb = len(bgp["banks"])
                    for _ in range(nb):
                        i = bgp["banks"][bgp["rr"] % nb]
                        bgp["rr"] += 1
                        if i not in bgp["held"]:
                            bgp["held"].add(i)
                            return i
                    raise RuntimeError("background PSUM pool exhausted")

                def bg_free(i):
                    bgp["held"].discard(i)

                class BG:
                    def __init__(self, width):
                        self.q, self.active, self.width = [], [], width

                    def add(self, g):
                        self.q.append(g)

                    def step(self):
                        for g in list(self.active):
                            try:
                                next(g)
                            except StopIteration:
                                self.active.remove(g)
                        if self.q and len(self.active) < self.width:
                            g = self.q.pop(0)
                            try:
                                next(g)
                                self.active.append(g)
                            except StopIteration:
                                pass

                    def drain(self):
                        while self.q or self.active:
                            self.step()

                bg = BG(2)
                cnq = [0]

                def add_k(h):
                    for tb in tbs_all:
                        bg.add(qk_chain(cnq[0], h, tb, False))
                        cnq[0] += 1
                    for t in range(NT):
                        bg.add(v_chain(h, t))

                def add_q(h, tb):
                    bg.add(qk_chain(cnq[0], h, tb, True, light=(h > 0)))
                    cnq[0] += 1

                load_w(0)
                add_k(0)
                for tb in q_tbs:
                    add_q(0, tb)
                bg.width = 3
                bg.drain()
                bg.width = 2
                bgp["banks"] = [4, 5, 6, 7]
                bgp["held"].clear()
                bgp["rr"] = 0
                pn = 0
                for h in range(6):
                    b = h % 2
                    if h + 1 < 6:
                        load_w(h + 1)
                        add_k(h + 1)
                    for tb in q_tbs:
                        t0, n_ = TBS[tb]
                        kts = [0, 1] if tb == 0 else list(range(NT))
                        po, pd = 0, 1
                        sct = [0]

                        def s_mm(kt):
                            ktb = tb_of_tile(kt)
                            ks = slice(kt * 128, (kt + 1) * 128)
                            psc = 2 + sct[0] % 2
                            sct[0] += 1
                            mm(PS[psc][:, :n_], KN[b][:, ks], QN[:, t0:t0 + n_], True, False,
                               [("KN", b, ktb), ("QN", tb)], [("ps", psc)])
                            mm(PS[psc][:, :n_], KR[b][:, ks], QR[:, t0:t0 + n_], False, True,
                               [("KR", b, ktb), ("QR", tb)], [("ps", psc)])
                            return psc
                        psc_next = s_mm(kts[0])
                        for n, kt in enumerate(kts):
                            psc = psc_next
                            if n + 1 < len(kts):
                                psc_next = s_mm(kts[n + 1])
                            pt = Pt[pn % 3]
                            pkey = ("Pt", pn % 3)
                            pn += 1
                            act(pt[:, :n_], PS[psc][:, :n_], AF.Exp, [("ps", psc)], [pkey], scale=SC)
                            mm(PS[po][:, :n_], Vt[b][:, kt, :], pt[:, :n_], n == 0, n == len(kts) - 1,
                               [("Vt", b, kt), pkey], [("ps", po)])
                            mm(PS[pd][:, :n_], ones_bf[:], pt[:, :n_], n == 0, n == len(kts) - 1, ["ones_bf", pkey],
                               [("ps", pd)])
                            bg.step()
                        recip_act(rec[:, :n_], PS[pd][:, :n_], [("ps", pd)], ["rec"])
                        vtt(MX[:, 2 + h, t0:t0 + n_], PS[po][:, :n_], rec[:, :n_], ALU.mult, [("ps", po), "rec"],
                            [("Z", 2 + h, tb)])
                        if h + 1 < 6:
                            add_q(h + 1, tb)
                    bg.drain()
                P.barrier()
            chk("attn_%d" % l)
            if dbg and ("mx%d" % l) in dbg_d:
                P.barrier()
                dump_fm(dbg_d["mx%d" % l], MX, 8, T)
                P.barrier()
            with ExitStack() as es4:
                mix_out(es4, l, mod_t, MX, "Z", tbs_res)
                P.barrier()

    main_mark = arena["cur"]
    try:
        main_layers()
    except _Stop:
        arena["cur"] = main_mark
        ps_held.clear()
        P.barrier()

    with ExitStack() as es:
        xo = [sb(es, "xo%d" % i, [128, D], F32) for i in range(2)]
        for n, t in enumerate(range(2, NT)):
            b = n % 2
            tb = tb_of_tile(t)
            for half in range(2):
                pi = ps_next()
                for q in range(4):
                    kc = half * 4 + q
                    tr(PS[pi][:, q * 128:(q + 1) * 128], ST[:, kc, t * 128:(t + 1) * 128], [("ST", kc, tb)],
                       [("ps", pi)])
                if half == 0:
                    vcopy(xo[b][:, 0:512], PS[pi][:, :], [("ps", pi)], [("xo", b, 0)])
                else:
                    act(xo[b][:, 512:1024], PS[pi][:, :], AF.Copy, [("ps", pi)], [("xo", b, 1)])
            dma("sync", out_d[(t - 2) * 128:(t - 1) * 128, :], xo[b][:], [("xo", b, 0), ("xo", b, 1)], [], "out%d" % b)
    if dbg and "st" in dbg_d:
        P.barrier()
        dump_fm(dbg_d["st"], ST, 8, T)
    if dbg and "z" in dbg_d:
        P.barrier()
        dump_fm(dbg_d["z"], Z, 8, T)
    P.emit(["out0", "out1"] + list(dbg_keys))
    top.close()
    print("SBUF arena peak bytes/partition:", arena["peak"], "of", a_size, "at", arena.get("peak_name"), "ops:", len(P.ops))
    return nc


def _consts():
    s = np.arange(128)
    same = (s[:, None] // 64) == (s[None, :] // 64)
    maskF = (same & (s[:, None] <= s[None, :])).astype(np.float32)
    maskB = (same & (s[:, None] >= s[None, :])).astype(np.float32)
    strictF = (same & (s[:, None] > s[None, :])).astype(np.float32)
    strictB = (same & (s[:, None] < s[None, :])).astype(np.float32)
    tri = np.stack([maskF * (-1.0 / 16), strictF * (-1.0 / 16), maskF,
                    maskB * (-1.0 / 16), strictB * (-1.0 / 16), maskB], axis=1).astype(np.float32)
    rows = SEQ // 64
    row_id = np.repeat(np.arange(rows, dtype=np.float64), 64)
    col_id = np.tile(np.arange(64, dtype=np.float64), rows)
    inv = 10000.0 ** (-np.arange(0, 32, 2, dtype=np.float64) / 32)
    ang_r = row_id[:, None] * inv
    ang_c = col_id[:, None] * inv
    ang = np.concatenate([ang_r, ang_r, ang_c, ang_c], axis=-1)
    rope = np.stack([np.cos(ang).T, np.sin(ang).T], axis=1).astype(np.float32)
    rmat = np.zeros((64, 64), np.float32)
    for blk in range(2):
        for q in range(16):
            a, b = blk * 32 + q, blk * 32 + 16 + q
            rmat[b, a] = -1.0
            rmat[a, b] = 1.0
    k64 = np.arange(64)
    a64 = 2 * np.pi * ((k64[:, None] * k64[None, :]) % 64) / 64
    c64, s64 = np.cos(a64), np.sin(a64)
    z = np.zeros((64, 64))
    c64blk = np.concatenate([np.block([[c64, z], [z, c64]]), np.block([[-s64, z], [z, -s64]])], axis=1).astype(np.float32)

    def dft(L):
        li = np.arange(L, dtype=np.int64)
        a = 2 * np.pi * ((li[:, None] * li[None, :]) % L).astype(np.float64) / L
        m = np.stack([np.cos(a), np.sin(a)], axis=1).astype(np.float32)
        return np.ascontiguousarray(m.reshape(L // 128, 128, 2, L)).astype(ml_dtypes.bfloat16)

    return dict(ident=np.eye(128, dtype=np.float32), tri=tri, rope_cs=rope, rmat=rmat, c64blk=c64blk,
                dft2048=dft(SEQ), dft256=dft(CTX))


def _fm(v, nchunk):
    return np.ascontiguousarray(np.asarray(v, np.float32).reshape(nchunk, 128).T)


def prep_inputs(inp, cores):
    g = {k: np.asarray(v) for k, v in inp.items()}
    shared = dict(_consts())
    shared["ada_w"] = g["ada_w"]
    shared["ada_b_fm"] = np.stack([_fm(g["ada_b"][l], 48) for l in range(DEPTH)])
    shared["ng_fm"] = np.stack([np.concatenate([_fm(g["norm_mix_g"][l], 8), _fm(g["norm_ffn_g"][l], 8)], axis=1)
                                for l in range(DEPTH)])
    for k in ("w_mix_out", "ffn_w_in", "ffn_w_out", "ev_w_in", "od_w_in", "mla_wuq", "mla_wukv"):
        shared[k] = g[k]
    wab = np.zeros((2, 33, 2, 256), np.float32)
    for i in range(2):
        for p in range(2):
            cs = slice(p * 128, (p + 1) * 128)
            wab[i, 0:16, p, 0:128] = g["gla_wa_f"][i][:, cs]
            wab[i, 16:32, p, 128:256] = g["gla_wa_b"][i][:, cs]
            wab[i, 32, p, 0:128] = g["gla_ba_f"][i][cs]
            wab[i, 32, p, 128:256] = g["gla_ba_b"][i][cs]
    shared["gla_wab"] = wab
    shared["gla_on_fm"] = np.ascontiguousarray(g["gla_onorm_g"].reshape(2, 128, 1))
    shared["sg_vng_bc"] = np.ascontiguousarray(np.broadcast_to(g["sg_vnorm_g"].reshape(2, 1, 512), (2, 128, 512)))
    shared["sg_wsT"] = np.ascontiguousarray(np.transpose(g["sg_ws"], (0, 3, 1, 2)))
    shared["sg_bs"] = np.ascontiguousarray(g["sg_bs"].reshape(2, 1, 512))
    shared["qa_g_fm"] = np.stack([_fm(g["mla_qa_g"][i], 3) for i in range(2)])
    shared["kva_g_fm"] = np.stack([_fm(g["mla_kva_g"][i], 2) for i in range(2)])
    qkn = np.zeros((2, 128, 4), np.float32)
    for i in range(2):
        qkn[i, :, 0] = g["mla_qn_g"][i][:128]
        qkn[i, :64, 1] = g["mla_qn_g"][i][128:]
        qkn[i, :, 2] = g["mla_kn_g"][i][:128]
        qkn[i, :64, 3] = g["mla_kn_g"][i][128:]
    shared["qkn_g_fm"] = qkn
    maps = []
    for b in cores:
        m = dict(shared)
        m["x"] = np.ascontiguousarray(g["x"][b])
        m["ctx"] = np.ascontiguousarray(g["ctx"][b])
        m["cvec"] = np.ascontiguousarray(np.stack([_fm(g["c"][b], 8), _fm(g["c_ctx"], 8)], axis=-1))
        maps.append(m)
    return maps


def kernel(**inputs):
    nc = build_program()
    maps = prep_inputs(inputs, list(range(8)))
    res = run_bass_kernel_spmd(nc, maps, core_ids=list(range(8)))
    return np.stack([np.asarray(r["out"], np.float32) for r in res.results], axis=0)
```

```python
import numpy as np
import ml_dtypes
from contextlib import ExitStack
import concourse.bass as bass
import concourse.mybir as mybir
from concourse.bass_utils import run_bass_kernel_spmd

F32 = mybir.dt.float32
BF16 = mybir.dt.bfloat16
AF = mybir.ActivationFunctionType
ALU = mybir.AluOpType

D = 1024
SEQ = 2048
CTX = 256
T = SEQ + CTX
NT = T // 128
O1W = 1
DEPTH = 4
FFN_H = 2816
EPS = 1e-6
TBS = [(0, 256)] + [(256 + 512 * i, 512) for i in range(4)]
ENGS = ["sync", "scalar", "vector", "gpsimd", "tensor"]


def tb_of_tile(t):
    return 0 if t < 2 else 1 + (t - 2) // 4


class _Op:
    __slots__ = ("eng", "fn", "deps", "sig", "num", "is_dma", "dkey", "dcum")


class Prog:
    def __init__(self, nc):
        self.nc = nc
        self.ops = []
        self.w = {}
        self.r = {}
        self.dma_cnt = {}
        self.last = {e: None for e in ENGS}
        self.pending_bar = {e: [] for e in ENGS}
        self.unwaited_dma = []

    def add(self, eng, fn, reads=(), writes=(), dkey=None):
        op = _Op()
        op.eng = eng
        op.fn = fn
        op.sig = False
        op.num = 0
        op.is_dma = dkey is not None
        op.dkey = dkey
        deps = {}
        for k in reads:
            for o in self.w.get(k, {}).values():
                deps[id(o)] = o
        for k in writes:
            for o in self.w.get(k, {}).values():
                deps[id(o)] = o
            for o in self.r.get(k, {}).values():
                deps[id(o)] = o
        for o in self.pending_bar[eng]:
            deps[id(o)] = o
        self.pending_bar[eng] = []
        dl = []
        for o in deps.values():
            if o is op:
                continue
            if (not o.is_dma) and o.eng == eng and eng == "tensor":
                continue
            if o.is_dma:
                dl.append((o, self.dma_cnt[o.dkey] * 16))
            else:
                dl.append((o, None))
        op.deps = dl
        if op.is_dma:
            self.dma_cnt[dkey] = self.dma_cnt.get(dkey, 0) + 1
            op.dcum = self.dma_cnt[dkey] * 16
        slot = id(op) if op.is_dma else eng
        for k in reads:
            self.r.setdefault(k, {})[slot] = op
        for k in writes:
            self.w[k] = {slot: op}
            self.r[k] = {}
        self.ops.append(op)
        if not op.is_dma:
            self.last[eng] = op
        else:
            self.unwaited_dma.append(op)
        return op

    def barrier(self):
        lasts = [o for o in self.last.values() if o is not None]
        dmas = list(self.unwaited_dma)
        self.unwaited_dma = []
        for e in ENGS:
            self.pending_bar[e] = self.pending_bar[e] + [o for o in lasts if o.eng != e or e != "tensor"] + dmas

    def emit(self, final_dma_keys):
        nc = self.nc
        for op in self.ops:
            for (o, _) in op.deps:
                if not o.is_dma:
                    o.sig = True
        cnt = {e: 0 for e in ENGS}
        for op in self.ops:
            if (not op.is_dma) and op.sig:
                cnt[op.eng] += 1
                op.num = cnt[op.eng]
        by_eng = {e: [o for o in self.ops if o.eng == e] for e in ENGS}
        with ExitStack() as es:
            esem = {e: es.enter_context(nc.semaphore("se_" + e)) for e in ENGS}
            dsem = {k: es.enter_context(nc.semaphore("sd_%d" % i)) for i, k in enumerate(self.dma_cnt)}
            block = es.enter_context(nc.Block())

            def make(eng):
                def body(e):
                    waited = {}
                    for op in by_eng[eng]:
                        for (o, v) in op.deps:
                            if o.is_dma:
                                sem, val, key = dsem[o.dkey], v, ("d", o.dkey)
                            else:
                                sem, val, key = esem[o.eng], o.num, ("e", o.eng)
                            if waited.get(key, 0) < val:
                                e.wait_ge(sem, val)
                                waited[key] = val
                        inst = op.fn(e)
                        if op.is_dma:
                            inst.then_inc(dsem[op.dkey], 16)
                        elif op.sig:
                            inst.then_inc(esem[eng], 1)
                    if eng == "sync":
                        for k in final_dma_keys:
                            e.wait_ge(dsem[k], self.dma_cnt[k] * 16)
                return body

            for eng in ENGS:
                getattr(block, eng)(make(eng))


class _Stop(Exception):
    pass


def build_program(n_layers=DEPTH, dbg=None, stop=None):
    nc = bass.Bass("TRN2", target_bir_lowering=False)
    P = Prog(nc)

    def din(name, shape, dt=F32):
        return nc.dram_tensor(name, list(shape), dt, kind="ExternalInput").ap()

    x_d = din("x", [SEQ, D])
    ctx_d = din("ctx", [CTX, D])
    cvec_d = din("cvec", [128, 8, 2])
    ada_w_d = din("ada_w", [DEPTH, D, 6 * D])
    ada_b_d = din("ada_b_fm", [DEPTH, 128, 48])
    ng_d = din("ng_fm", [DEPTH, 128, 16])
    wmo_d = din("w_mix_out", [DEPTH, D, D])
    fwi_d = din("ffn_w_in", [DEPTH, D, 2 * FFN_H])
    fwo_d = din("ffn_w_out", [DEPTH, FFN_H, D])
    evw_d = din("ev_w_in", [2, D, 2592])
    wab_d = din("gla_wab", [2, 33, 2, 256])
    gon_d = din("gla_on_fm", [2, 128, 1])
    vng_d = din("sg_vng_bc", [2, 128, 512])
    wst_d = din("sg_wsT", [2, 128, 4, 128])
    sbs_d = din("sg_bs", [2, 1, 512])
    odw_d = din("od_w_in", [2, D, 960])
    qag_d = din("qa_g_fm", [2, 128, 3])
    kvg_d = din("kva_g_fm", [2, 128, 2])
    qkn_d = din("qkn_g_fm", [2, 128, 4])
    wuq_d = din("mla_wuq", [2, 384, 1152])
    wukv_d = din("mla_wukv", [2, 256, 1536])
    ident_d = din("ident", [128, 128])
    tri_d = din("tri", [128, 6, 128])
    rope_d = din("rope_cs", [64, 2, SEQ])
    rmat_d = din("rmat", [64, 64])
    c64_d = din("c64blk", [128, 256])
    dftL_d = din("dft2048", [16, 128, 2, SEQ], BF16)
    dftC_d = din("dft256", [2, 128, 2, CTX], BF16)
    out_d = nc.dram_tensor("out", [SEQ, D], F32, kind="ExternalOutput").ap()
    dbg_d = {}
    if dbg:
        for name, shape in dbg.items():
            dbg_d[name] = nc.dram_tensor("dbg_" + name, list(shape), F32, kind="ExternalOutput").ap()

    top = ExitStack()
    a_base = (nc.sbuf_base + 31) // 32 * 32
    a_size = nc.sbuf_top - a_base - 2048
    nc.alloc_sbuf_tensor("arena", [128, a_size], mybir.dt.uint8)
    arena = {"cur": a_base, "n": 0, "peak": 0}

    def _rel(mark):
        arena["cur"] = mark

    def sb(es, name, shape, dt):
        if not hasattr(es, "_mark"):
            es._mark = arena["cur"]
            es.callback(_rel, es._mark)
        nb = int(np.prod(shape[1:])) * (4 if dt == F32 else 2)
        nb = (nb + 63) // 64 * 64
        off = arena["cur"]
        assert off + nb <= a_base + a_size, ("SBUF arena overflow", name, off + nb - a_base, a_size)
        arena["cur"] = off + nb
        if arena["cur"] - a_base > arena["peak"]:
            arena["peak"] = arena["cur"] - a_base
            arena["peak_name"] = name
        arena["n"] += 1
        return nc.alloc_sbuf_tensor_at("%s_%d" % (name, arena["n"]), list(shape), dt, offset=off)

    ST = sb(top, "ST", [128, 8, T], F32)
    Z = sb(top, "Z", [128, 8, T], BF16)
    ident = sb(top, "ident", [128, 128], F32)
    ones_bf = sb(top, "ones_bf", [128, 128], BF16)
    silu_c = sb(top, "silu_c", [128, 8, 2], BF16)
    MOD = [sb(top, "MOD%d" % i, [128, 48, 2], F32) for i in range(2)]
    AB = sb(top, "AB", [128, 2, 8, 2], F32)
    ng = sb(top, "ng", [128, DEPTH, 16], F32)
    adab = sb(top, "adab", [128, DEPTH, 48], F32)
    PS = [top.enter_context(nc.psum_tensor("ps%d" % i, [128, 512], F32)) for i in range(8)]
    ps_rr = [0]

    ps_held = set()

    def ps_next(hold=False):
        for _ in range(8):
            i = ps_rr[0]
            ps_rr[0] = (i + 1) % 8
            if i not in ps_held:
                if hold:
                    ps_held.add(i)
                return i
        raise RuntimeError("all PSUM banks held")

    def ps_release(i):
        ps_held.discard(i)

    def dma(eng, out, in_, reads, writes, dkey):
        return P.add(eng, lambda e: e.dma_start(out=out, in_=in_), reads, writes, dkey=dkey)

    def mm(out, lhsT, rhs, start, stop, reads, writes):
        return P.add("tensor", lambda e: e.matmul(out, lhsT=lhsT, rhs=rhs, start=start, stop=stop), reads, writes)

    def tr(out, in_, reads, writes):
        return P.add("tensor", lambda e: e.transpose(out, in_, ident[:]), list(reads) + ["ident"], writes)

    def act(out, in_, func, reads, writes, bias=None, scale=None, accum_out=None):
        kw = {}
        if bias is not None:
            kw["bias"] = bias
        if scale is not None:
            kw["scale"] = scale
        if accum_out is not None:
            kw["accum_out"] = accum_out
        return P.add("scalar", lambda e: e.activation(out=out, in_=in_, func=func, **kw), reads, writes)

    def vtt(out, in0, in1, op, reads, writes, eng="vector"):
        return P.add(eng, lambda e: e.tensor_tensor(out=out, in0=in0, in1=in1, op=op), reads, writes)

    def vstt(out, in0, scalar, in1, op0, op1, reads, writes):
        return P.add("vector", lambda e: e.scalar_tensor_tensor(out=out, in0=in0, scalar=scalar, in1=in1,
                                                                 op0=op0, op1=op1), reads, writes)

    def vts(out, in0, s1, s2, op0, op1, reads, writes, eng="vector"):
        if op1 is None:
            return P.add(eng, lambda e: e.tensor_scalar(out=out, in0=in0, scalar1=s1, scalar2=None, op0=op0),
                         reads, writes)
        return P.add(eng, lambda e: e.tensor_scalar(out=out, in0=in0, scalar1=s1, scalar2=s2, op0=op0, op1=op1),
                     reads, writes)

    def vcopy(out, in_, reads, writes, eng="vector"):
        return P.add(eng, lambda e: e.tensor_copy(out=out, in_=in_), reads, writes)

    def vrecip(out, in_, reads, writes):
        return P.add("vector", lambda e: e.reciprocal(out=out, in_=in_), reads, writes)

    def memset(eng, ap, val, writes):
        return P.add(eng, lambda e: e.memset(ap, val), [], writes)

    def rsqrt_from(out_f32, in_ap, scale, reads, writes, tmpkey=None):
        act(out_f32, in_ap, AF.Ln, reads, writes, bias=eps_t[:out_f32.shape[0], :], scale=scale)
        act(out_f32, out_f32, AF.Exp, writes, writes, scale=-0.5)

    def recip_act(out_f32, in_ap, reads, writes):
        act(out_f32, in_ap, AF.Ln, reads, writes)
        act(out_f32, out_f32, AF.Exp, writes, writes, scale=-1.0)

    dbg_keys = []

    def dump_fm(dst, buf, nch, ntok):
        for kc in range(nch):
            for h0 in range(0, ntok, 1024):
                h1 = min(ntok, h0 + 1024)
                k = "dbg%d" % len(dbg_keys)
                dbg_keys.append(k)
                P.add("gpsimd", lambda e, kc=kc, h0=h0, h1=h1: e.dma_start(out=dst[kc, :, h0:h1], in_=buf[:, kc, h0:h1]),
                      [], [], dkey=k)

    def run_staggered(gens, width=64):
        gens = list(gens)
        active = []
        nxt = 0
        while nxt < len(gens) or active:
            for g in list(active):
                try:
                    next(g)
                except StopIteration:
                    active.remove(g)
            if nxt < len(gens) and len(active) < width:
                g = gens[nxt]
                nxt += 1
                try:
                    next(g)
                    active.append(g)
                except StopIteration:
                    pass

    eps_t = sb(top, "eps_t", [128, 1], F32)
    one_t = sb(top, "one_t", [128, 1], F32)
    memset("vector", eps_t[:], EPS, ["eps"])
    memset("vector", one_t[:], 1.0, ["one"])
    memset("vector", ones_bf[:], 1.0, ["ones_bf"])
    ident_bf = sb(top, "ident_bf", [128, 128], BF16)
    rowm = sb(top, "rowm", [128, 2], F32)
    memset("vector", rowm[:], 0.0, ["rowm"])
    memset("vector", rowm[0:64, 0:1], 1.0, ["rowm"])
    memset("vector", rowm[64:128, 1:2], 1.0, ["rowm"])
    dma("sync", ident[:], ident_d[:, :], [], ["ident"], "c_ident")
    vcopy(ident_bf[:], ident[:], ["ident"], ["ident_bf"])
    dma("sync", ng[:], ng_d.rearrange("l p k -> p l k"), [], ["ng"], "c_ng")
    dma("sync", adab[:], ada_b_d.rearrange("l p k -> p l k"), [], ["adab"], "c_adab")

    with ExitStack() as es:
        xin = [sb(es, "xin%d" % i, [128, D], F32) for i in range(3)]
        cv = sb(es, "cv", [128, 8, 2], F32)
        dma("sync", cv[:], cvec_d[:, :, :], [], ["cv"], "c_cv")
        act(silu_c[:], cv[:], AF.Silu, ["cv"], ["silu_c"])
        for t in range(NT):
            src = ctx_d[t * 128:(t + 1) * 128, :] if t < 2 else x_d[(t - 2) * 128:(t - 1) * 128, :]
            b = t % 3
            dma("sync", xin[b][:], src, [], [("xin", b)], "xin%d" % b)
            tb = tb_of_tile(t)
            for half in range(2):
                pi = ps_next()
                for q in range(4):
                    kc = half * 4 + q
                    tr(PS[pi][:, q * 128:(q + 1) * 128], xin[b][:, kc * 128:(kc + 1) * 128],
                       [("xin", b)], [("ps", pi)])
                dst = ST[:, half * 4:half * 4 + 4, t * 128:(t + 1) * 128]
                srcp = PS[pi][:, :].rearrange("p (a b) -> p a b", a=4)
                wk = [("ST", half * 4 + q, tb) for q in range(4)]
                if half == 0:
                    vcopy(dst, srcp, [("ps", pi)], wk)
                else:
                    act(dst, srcp, AF.Copy, [("ps", pi)], wk)
        P.barrier()

    def mod_block(es_bufs, l, blk, mod_t):
        buf = es_bufs[blk % 2]
        key = ("adaw", blk % 2)
        dma("gpsimd", buf[:], ada_w_d[l, :, blk * 512:(blk + 1) * 512].rearrange("(kc p) n -> p kc n", p=128),
            [], [key], "adaw%d" % (blk % 2))
        pi = ps_next()
        for q in range(4):
            fc = blk * 4 + q
            for kc in range(8):
                mm(PS[pi][:, q * 2:q * 2 + 2], buf[:, kc, q * 128:(q + 1) * 128], silu_c[:, kc, :],
                   kc == 0, kc == 7, [key, "silu_c"], [("ps", pi)])
        for j in range(2):
            o = mod_t[:, blk * 4:blk * 4 + 4, j]
            i0 = PS[pi][:, 0:8].rearrange("p (a b) -> p a b", b=2)[:, :, j]
            vtt(o, i0, adab[:, l, blk * 4:blk * 4 + 4], ALU.add, [("ps", pi), "adab"], [("mod", l % 2, blk)])

    def mod_finish(l, mod_t, which=(0, 1)):
        for n, sc0 in ((0, 8), (1, 32)):
            if n not in which:
                continue
            for j in range(2):
                P.add("vector", lambda e, n=n, sc0=sc0, j=j: e.scalar_tensor_tensor(
                    out=AB[:, n, :, j], in0=mod_t[:, sc0:sc0 + 8, j], scalar=1.0, in1=ng[:, l, n * 8:(n + 1) * 8],
                    op0=ALU.add, op1=ALU.mult),
                    [("mod", l % 2, b) for b in (sc0 // 4, sc0 // 4 + 1)] + ["ng"], [("AB", n)])

    def norm_bufs(es, tagp):
        sq = [sb(es, tagp + "sq%d" % i, [128, 8, 512], BF16) for i in range(2)]
        rs = [sb(es, tagp + "rs%d" % i, [128, 512], F32) for i in range(2)]
        tmp = [sb(es, tagp + "tmp%d" % i, [128, 512], F32) for i in range(3)]
        return (sq, rs, tmp, tagp)

    def norm_mod(nb, l, n, mod_t, tbs):
        sq, rs, tmp, tagp = nb
        sh0 = 0 if n == 0 else 24
        tcnt = [0]

        def gen(ci, tb):
            t0, n_ = TBS[tb]
            j = 1 if tb == 0 else 0
            sl = ci % 2
            act(sq[sl][:, 0:5, :n_], ST[:, 0:5, t0:t0 + n_], AF.Square, [("ST", kc, tb) for kc in range(5)],
                [(tagp + "sqa", sl)])
            vtt(sq[sl][:, 5:8, :n_], ST[:, 5:8, t0:t0 + n_], ST[:, 5:8, t0:t0 + n_], ALU.mult,
                [("ST", kc, tb) for kc in range(5, 8)], [(tagp + "sqb", sl)])
            yield
            pi = ps_next()
            for kc in range(8):
                mm(PS[pi][:, :n_], ones_bf[:], sq[sl][:, kc, :n_], kc == 0, kc == 7,
                   [(tagp + ("sqa" if kc < 5 else "sqb"), sl), "ones_bf"], [("ps", pi)])
            rsqrt_from(rs[sl][:, :n_], PS[pi][:, :n_], 1.0 / D, [("ps", pi), "eps"], [(tagp + "rs", sl)])
            yield
            for kc in range(8):
                ti = tcnt[0] % 3
                tcnt[0] += 1
                tt = tmp[ti]
                vstt(tt[:, :n_], ST[:, kc, t0:t0 + n_], AB[:, n, kc, j:j + 1], rs[sl][:, :n_], ALU.mult, ALU.mult,
                     [("ST", kc, tb), ("AB", n), (tagp + "rs", sl)], [(tagp + "tmp", ti)])
                act(Z[:, kc, t0:t0 + n_], tt[:, :n_], AF.Identity, [(tagp + "tmp", ti),
                    ("mod", l % 2, (sh0 + kc) // 4)], [("Z", kc, tb)],
                    bias=mod_t[:, sh0 + kc, j:j + 1], scale=1.0)
                if kc == 3:
                    yield

        run_staggered([gen(ci, tb) for ci, tb in enumerate(tbs)])

    def wload(buf, key, dkey, src):
        return dma("gpsimd", buf, src, [], [key], dkey)

    def proj_fm(wbuf, wkey, c0, m, tb, evac):
        t0, n_ = TBS[tb]
        pi = ps_next()
        for kc in range(8):
            mm(PS[pi][:m, :n_], wbuf[:, kc, c0:c0 + m], Z[:, kc, t0:t0 + n_], kc == 0, kc == 7,
               [wkey, ("Z", kc, tb)], [("ps", pi)])
        evac(pi, tb, t0, n_)

    def proj_tm(wbuf, wkey, c0, ncol, t, evac):
        tb = tb_of_tile(t)
        pi = ps_next()
        for kc in range(8):
            mm(PS[pi][:, :ncol], Z[:, kc, t * 128:(t + 1) * 128], wbuf[:, kc, c0:c0 + ncol], kc == 0, kc == 7,
               [wkey, ("Z", kc, tb)], [("ps", pi)])
        evac(pi, t)

    def mix_out(es, l, mod_t, MX, mxname, tbs):
        wb = [sb(es, "wmo%d" % i, [128, 8, 512], BF16) for i in range(2)]
        for hf in range(2):
            wload(wb[hf][:], ("wmo", hf), "wmo%d" % hf,
                  wmo_d[l, :, hf * 512:(hf + 1) * 512].rearrange("(kc p) n -> p kc n", p=128))
        for oc in range(8):
            hf, c0 = oc // 4, (oc % 4) * 128
            for tb in tbs:
                t0, n_ = TBS[tb]
                j = 1 if tb == 0 else 0
                pi = ps_next()
                for kc in range(8):
                    mm(PS[pi][:, :n_], wb[hf][:, kc, c0:c0 + 128], MX[:, kc, t0:t0 + n_], kc == 0, kc == 7,
                       [("wmo", hf), (mxname, kc, tb)], [("ps", pi)])
                vstt(ST[:, oc, t0:t0 + n_], PS[pi][:, :n_], mod_t[:, 16 + oc, j:j + 1], ST[:, oc, t0:t0 + n_],
                     ALU.mult, ALU.add, [("ps", pi), ("ST", oc, tb)] + [("mod", l % 2, b) for b in range(12)],
                     [("ST", oc, tb)])

    def ffn(es, l, mod_t, tbs, next_mod):
        G = 2
        NG = 22 // G
        w1 = [sb(es, "fw1_%d" % i, [128, 8, 4 * 128], BF16) for i in range(2)]
        w2 = [sb(es, "fw2_%d" % i, [128, G, D], BF16) for i in range(2)]
        actb = [sb(es, "fact%d" % i, [128, G, T], BF16) for i in range(2)]
        sg = [sb(es, "fsg%d" % i, [128, 512], F32) for i in range(2)]
        adabuf = None
        if next_mod is not None:
            adabuf = [sb(es, "adaw%d" % i, [128, 8, 512], BF16) for i in range(2)]

        def load(gi):
            b = gi % 2
            h0 = gi * G * 128
            wload(w1[b][:, :, 0:256], ("fw1g", b), "fw1g%d" % b,
                  fwi_d[l, :, h0:h0 + 256].rearrange("(kc p) n -> p kc n", p=128))
            wload(w1[b][:, :, 256:512], ("fw1u", b), "fw1u%d" % b,
                  fwi_d[l, :, FFN_H + h0:FFN_H + h0 + 256].rearrange("(kc p) n -> p kc n", p=128))
            wload(w2[b][:], ("fw2", b), "fw2_%d" % b,
                  fwo_d[l, h0:h0 + 256, :].rearrange("(g p) n -> p g n", p=128))

        load(0)
        sgi = 0
        for gi in range(NG):
            b = gi % 2
            if gi + 1 < NG:
                load(gi + 1)
            for jj in range(G):
                for tb in tbs:
                    t0, n_ = TBS[tb]
                    pg = ps_next()
                    for kc in range(8):
                        mm(PS[pg][:, :n_], w1[b][:, kc, jj * 128:(jj + 1) * 128], Z[:, kc, t0:t0 + n_], kc == 0,
                           kc == 7, [("fw1g", b), ("Z", kc, tb)], [("ps", pg)])
                    pu = ps_next()
                    for kc in range(8):
                        mm(PS[pu][:, :n_], w1[b][:, kc, 256 + jj * 128:256 + (jj + 1) * 128], Z[:, kc, t0:t0 + n_],
                           kc == 0, kc == 7, [("fw1u", b), ("Z", kc, tb)], [("ps", pu)])
                    s = sg[sgi % 2]
                    skey = ("fsg", sgi % 2)
                    sgi += 1
                    act(s[:, :n_], PS[pg][:, :n_], AF.Silu, [("ps", pg)], [skey])
                    vtt(actb[b][:, jj, t0:t0 + n_], PS[pu][:, :n_], s[:, :n_], ALU.mult, [("ps", pu), skey],
                        [("fact", b, jj, tb)])
            for tb in tbs:
                for oc in range(8):
                    t0, n_ = TBS[tb]
                    j = 1 if tb == 0 else 0
                    pi = ps_next()
                    for jj in range(G):
                        mm(PS[pi][:, :n_], w2[b][:, jj, oc * 128:(oc + 1) * 128], actb[b][:, jj, t0:t0 + n_],
                           jj == 0, jj == G - 1, [("fw2", b), ("fact", b, jj, tb)], [("ps", pi)])
                    vstt(ST[:, oc, t0:t0 + n_], PS[pi][:, :n_], mod_t[:, 40 + oc, j:j + 1], ST[:, oc, t0:t0 + n_],
                         ALU.mult, ALU.add, [("ps", pi), ("ST", oc, tb)] + [("mod", l % 2, bb) for bb in range(12)],
                         [("ST", oc, tb)])
            if next_mod is not None:
                nl, nmod = next_mod
                mod_block(adabuf, nl, gi, nmod)
                if gi == NG - 1:
                    mod_block(adabuf, nl, 11, nmod)

    def sg_mixer(es0, l, MX, tbs, tiles):
        i = l // 2
        with ExitStack() as es:
            VN = sb(es, "VN", [128, NT, 512], BF16)
            wu = [sb(es, "wu%d" % k, [128, 8, 512], BF16) for k in range(2)]
            vng = sb(es, "vng", [128, 512], F32)
            wsT = sb(es, "wsT", [128, 4, 128], BF16)
            bsr = sb(es, "bsr", [1, 512], BF16)
            gsv = [sb(es, "gsv%d" % k, [128, 512], F32) for k in range(4)]
            junk = [sb(es, "sgjunk%d" % k, [128, 512], F32) for k in range(2)]
            ssq = [sb(es, "ssq%d" % k, [128, 4], F32) for k in range(4)]
            dma("sync", vng[:], vng_d[i, :, :], [], ["vng"], "c_vng")
            wload(wsT[:], "wsT", "c_wsT", wst_d[i, :, :, :])
            wload(bsr[:], "bsr", "c_bsr", sbs_d[i, :, :])
            wload(wu[0][:], ("wu", 0), "wu0", evw_d[i, :, 1568:2080].rearrange("(kc p) n -> p kc n", p=128))
            wload(wu[1][:], ("wu", 1), "wu1", evw_d[i, :, 2080:2592].rearrange("(kc p) n -> p kc n", p=128))
            for g in range(4):
                for tb in tbs:
                    def ev(pi, tb, t0, n_, g=g):
                        act(MX[:, 4 + g, t0:t0 + n_], PS[pi][:, :n_], AF.Gelu, [("ps", pi)], [("MX", 4 + g, tb)])
                    proj_fm(wu[0], ("wu", 0), g * 128, 128, tb, ev)
            def sv_gen(n, t):
                tb = tb_of_tile(t)
                sl = n % 4
                gs = gsv[sl]
                gk = ("gsv", sl)
                sk = ("ssq", sl)
                pi = ps_next()
                for kc in range(8):
                    mm(PS[pi][:, :512], Z[:, kc, t * 128:(t + 1) * 128], wu[1][:, kc, 0:512], kc == 0, kc == 7,
                       [("wu", 1), ("Z", kc, tb)], [("ps", pi)])
                act(gs[:], PS[pi][:, :], AF.Gelu, [("ps", pi)], [gk])
                yield
                jk = junk[n % 2]
                vtt(jk[:], gs[:], gs[:], ALU.mult, [gk], [("sgjunk", n % 2)])
                P.add("vector", lambda e: e.tensor_reduce(out=ssq[sl][:, :], in_=jk[:, :].rearrange("p (a b) -> p a b", a=4),
                                                          axis=mybir.AxisListType.X, op=ALU.add),
                      [("sgjunk", n % 2)], [sk])
                rsqrt_from(ssq[sl][:, :], ssq[sl][:, :], 1.0 / 128, [sk, "eps"], [sk])
                yield
                for g in range(4):
                    vstt(VN[:, t, g * 128:(g + 1) * 128], gs[:, g * 128:(g + 1) * 128], ssq[sl][:, g:g + 1],
                         vng[:, g * 128:(g + 1) * 128], ALU.mult, ALU.mult, [gk, sk, "vng"], [("VN", t)])
                yield
                pi = ps_next()
                for g in range(4):
                    mm(PS[pi][:, g * 128:(g + 1) * 128], VN[:, t, g * 128:(g + 1) * 128], wsT[:, g, :], True, False,
                       [("VN", t), "wsT"], [("ps", pi)])
                    mm(PS[pi][:, g * 128:(g + 1) * 128], ones_bf[0:1, :], bsr[0:1, g * 128:(g + 1) * 128], False,
                       True, ["ones_bf", "bsr"], [("ps", pi)])
                yield
                dst = MX[:, 4:8, t * 128:(t + 1) * 128]
                vtt(dst, PS[pi][:, :].rearrange("p (a b) -> p a b", a=4), dst, ALU.mult,
                    [("ps", pi)] + [("MX", 4 + g, tb) for g in range(4)], [("MX", 4 + g, tb) for g in range(4)])

            run_staggered([sv_gen(n, t) for n, t in enumerate(tiles)])
            P.barrier()

    def gla_mixer(es0, l, MX, tbs, tiles, tri, A_T):
        i = l // 2
        gon = sb(es0, "gon", [128, 1], F32)
        wab = sb(es0, "wab", [33, 2, 256], BF16)
        dma("sync", gon[:], gon_d[i, :, :], [], ["gon"], "c_gon")
        wload(wab[:], "wab", "c_wab", wab_d[i, :, :, :])
        ctx_tiles = [t for t in tiles if t < 2]
        lat_tiles = [t for t in tiles if t >= 2]
        for p in range(2):
            with ExitStack() as es:
                QT = sb(es, "QT", [128, T], BF16)
                KT = sb(es, "KT", [128, T], BF16)
                Ktok = sb(es, "Ktok", [128, NT, 128], BF16)
                Vtok = sb(es, "Vtok", [128, NT, 256], BF16)
                with ExitStack() as es2:
                    wq = sb(es2, "wq", [128, 8, 128], BF16)
                    wk = sb(es2, "wk", [128, 8, 128], BF16)
                    wv = sb(es2, "wv", [128, 8, 256], BF16)
                    wg = sb(es2, "wg", [128, 8, 256], BF16)
                    wa = sb(es2, "wa", [128, 8, 32], BF16)
                    rr = lambda c0, n: evw_d[i, :, c0:c0 + n].rearrange("(kc p) n -> p kc n", p=128)
                    wload(wq[:], "wq", "wq", rr(p * 128, 128))
                    wload(wk[:], "wk", "wk", rr(256 + p * 128, 128))
                    wload(wv[:], "wv", "wv", rr(512 + p * 256, 256))
                    wload(wg[:], "wg", "wg", rr(1024 + p * 256, 256))
                    if p == 0:
                        wload(wa[:], "wa", "wa", rr(1536, 32))
                    for tb in tbs:
                        proj_fm(wq, "wq", 0, 128, tb, lambda pi, tb, t0, n_: act(
                            QT[:, t0:t0 + n_], PS[pi][:, :n_], AF.Copy, [("ps", pi)], [("QT", tb)], scale=0.125))
                        proj_fm(wk, "wk", 0, 128, tb, lambda pi, tb, t0, n_: vcopy(
                            KT[:, t0:t0 + n_], PS[pi][:, :n_], [("ps", pi)], [("KT", tb)]))
                        for hh in range(2):
                            proj_fm(wg, "wg", hh * 128, 128, tb, lambda pi, tb, t0, n_, hh=hh: act(
                                MX[:, 2 * p + hh, t0:t0 + n_], PS[pi][:, :n_], AF.Silu, [("ps", pi)],
                                [("MX", 2 * p + hh, tb)]))
                        if p == 0:
                            proj_fm(wa, "wa", 0, 32, tb, lambda pi, tb, t0, n_: vcopy(
                                A_T[0:32, t0:t0 + n_], PS[pi][:32, :n_], [("ps", pi)], [("A_T", tb)]))
                    for t in tiles:
                        proj_tm(wk, "wk", 0, 128, t, lambda pi, t: vcopy(
                            Ktok[:, t, :], PS[pi][:, :128], [("ps", pi)], [("Ktok", t)]))
                        proj_tm(wv, "wv", 0, 256, t, lambda pi, t: act(
                            Vtok[:, t, :], PS[pi][:, :256], AF.Copy, [("ps", pi)], [("Vtok", t)]))
                    P.barrier()
                with ExitStack() as es2:
                    OF = sb(es2, "OF", [128, 2, T], BF16)
                    es_sc = ExitStack()
                    es2_outer = es2
                    es2 = es_sc
                    Sf = [sb(es2, "Sf%d" % k, [128, 128], F32) for k in range(2)]
                    Sb = [sb(es2, "Sb%d" % k, [128, 128], BF16) for k in range(8)]
                    def slots(name, n, shape, dt):
                        return [sb(es2, "%s%d" % (name, k), shape, dt) for k in range(n)]
                    Gt = slots("Gt", 2, [128, 128], BF16)
                    Et = slots("Et", 2, [128, 128], F32)
                    EB = slots("EB", 6, [128, 128], F32)
                    ENB = slots("ENB", 2, [128, 128], F32)
                    ED = slots("ED", 2, [128, 128], F32)
                    QE = [slots("QE%d_" % hh, 6, [128, 128], BF16) for hh in range(2)]
                    KE = slots("KE", 2, [128, 128], BF16)
                    KL = [slots("KL%d_" % c, 2, [128, 128], BF16) for c in range(2)]
                    ATs = [slots("ATs%d_" % hh, 2, [128, 128], BF16) for hh in range(2)]

                    def pick(lst, n):
                        i = n % len(lst)
                        return lst[i], i

                    for d in range(2):
                        order = (ctx_tiles + lat_tiles) if d == 0 else (ctx_tiles[::-1] + lat_tiles[::-1])
                        tri_incl = tri[:, 0 + 3 * d, :]
                        tri_strict = tri[:, 1 + 3 * d, :]
                        mask = tri[:, 2 + 3 * d, :]
                        memset("vector", Sf[0][:], 0.0, [("Sf", 0)])
                        memset("vector", Sb[0][:], 0.0, [("Sb", 0)])
                        sfi = [0]
                        sbi = [0]
                        chunks = (0, 1) if d == 0 else (1, 0)

                        def tile_gen(n, t):
                            tb = tb_of_tile(t)
                            tok = slice(t * 128, (t + 1) * 128)
                            et, eti = pick(Et, n)
                            gt, gti = pick(Gt, n)
                            eb, ebi = pick(EB, n)
                            enb, enbi = pick(ENB, n)
                            ed, edi = pick(ED, n)
                            ke, kei = pick(KE, n)
                            qe = [pick(QE[hh], n) for hh in range(2)]
                            kl = [pick(KL[c], n) for c in range(2)]
                            ats = [pick(ATs[hh], n) for hh in range(2)]
                            pp = 0
                            mm(PS[pp][:, 0:128], A_T[0:33, tok], wab[0:33, p, d * 128:(d + 1) * 128], True, True,
                               [("A_T", tb), "A_T1", "wab"], [("ps", pp)])
                            act(et[:], PS[pp][:, 0:128], AF.Exp, [("ps", pp)], [("Et", eti)], scale=-1.0)
                            act(gt[:], et[:], AF.Ln, [("Et", eti), "one"], [("Gt", gti)], bias=one_t[:, :], scale=1.0)
                            yield
                            pb = 1
                            mm(PS[pb][:, 0:128], gt[:], tri_incl, True, True, [("Gt", gti), "tri"], [("ps", pb)])
                            mm(PS[pb][:, 128:256], tri_strict, gt[:], True, True, [("Gt", gti), "tri"], [("ps", pb)])
                            act(eb[:], PS[pb][:, 0:128], AF.Exp, [("ps", pb)], [("EB", ebi)])
                            act(enb[:], PS[pb][:, 0:128], AF.Exp, [("ps", pb)], [("ENB", enbi)], scale=-1.0)
                            act(ed[:], PS[pb][:, 128:256], AF.Exp, [("ps", pb)], [("ED", edi)])
                            yield
                            for hh in range(2):
                                vstt(qe[hh][0][:], QT[:, tok], rowm[:, hh:hh + 1], eb[:], ALU.mult, ALU.mult,
                                     [("QT", tb), ("EB", ebi), "rowm"], [("QE", hh, qe[hh][1])])
                            vtt(ke[:], KT[:, tok], enb[:], ALU.mult, [("KT", tb), ("ENB", enbi)], [("KE", kei)])
                            for c in range(2):
                                vstt(kl[c][0][:], Ktok[:, t, :], rowm[:, c:c + 1], ed[:], ALU.mult, ALU.mult,
                                     [("Ktok", t), ("ED", edi), "rowm"], [("KL", c, kl[c][1])])
                            yield
                            pS = 5 + n % 3
                            for c in range(2):
                                for hh in range(2):
                                    hs = slice(hh * 64, (hh + 1) * 64)
                                    mm(PS[pS][hs, c * 128:(c + 1) * 128], kl[c][0][:, hs],
                                       Vtok[:, t, hh * 128:(hh + 1) * 128], True, True,
                                       [("KL", c, kl[c][1]), ("Vtok", t)], [("ps", pS)])
                            pa = 2
                            for hh in range(2):
                                mm(PS[pa][:, hh * 128:(hh + 1) * 128], ke[:, :], qe[hh][0][:, :], True, True,
                                   [("KE", kei), ("QE", hh, qe[hh][1])], [("ps", pa)])
                            for hh in range(2):
                                vtt(ats[hh][0][:], PS[pa][:, hh * 128:(hh + 1) * 128], mask, ALU.mult, [("ps", pa), "tri"],
                                    [("ATs", hh, ats[hh][1])])
                            yield
                            pq = 3
                            for hh in range(2):
                                hc = slice(hh * 128, (hh + 1) * 128)
                                if d == 1:
                                    mm(PS[pq][:, hc], ident_bf[:], OF[:, hh, tok], True, False,
                                       ["ident_bf", ("OF", hh, t)], [("ps", pq)])
                                mm(PS[pq][:, hc], Vtok[:, t, hh * 128:(hh + 1) * 128], ats[hh][0][:], d == 0, True,
                                   [("Vtok", t), ("ATs", hh, ats[hh][1])], [("ps", pq)])
                            act(OF[:, :, tok], PS[pq][:, 0:256].rearrange("p (a b) -> p a b", a=2), AF.Copy,
                                [("ps", pq)], [("OF", 0, t), ("OF", 1, t)])
                            yield
                            snaps = []
                            for c in chunks:
                                last_col = (c * 64 + 63) if d == 0 else (c * 64)
                                a, b2 = sfi[0], 1 - sfi[0]
                                sfi[0] = b2
                                jp = sbi[0]
                                jn = (jp + 1) % len(Sb)
                                sbi[0] = jn
                                snaps.append((c, jp))
                                vstt(Sf[b2][:], Sf[a][:], eb[:, last_col:last_col + 1],
                                     PS[pS][:, c * 128:(c + 1) * 128], ALU.mult, ALU.add,
                                     [("Sf", a), ("EB", ebi), ("ps", pS)], [("Sf", b2)])
                                vstt(Sb[jn][:], Sf[a][:], eb[:, last_col:last_col + 1],
                                     PS[pS][:, c * 128:(c + 1) * 128], ALU.mult, ALU.add,
                                     [("Sf", a), ("EB", ebi), ("ps", pS)], [("Sb", jn)])
                            yield
                            pI = 4
                            mm(PS[pI][:, 0:256].rearrange("p (a b) -> p a b", a=2), ident_bf[:], OF[:, :, tok], True, False,
                               ["ident_bf", ("OF", 0, t), ("OF", 1, t)], [("ps", pI)])
                            for ci_, (c, j) in enumerate(snaps):
                                cs = slice(c * 64, (c + 1) * 64)
                                for hh in range(2):
                                    mm(PS[pI][:, hh * 128 + c * 64:hh * 128 + (c + 1) * 64], Sb[j][:, :],
                                       qe[hh][0][:, cs], False, ci_ == 1 and hh == 1,
                                       [("Sb", j), ("QE", hh, qe[hh][1])], [("ps", pI)])
                            act(OF[:, :, tok], PS[pI][:, 0:256].rearrange("p (a b) -> p a b", a=2), AF.Copy,
                                [("ps", pI)], [("OF", 0, t), ("OF", 1, t)])

                        run_staggered([tile_gen(n, t) for n, t in enumerate(order)])
                    P.barrier()
                    es_sc.close()
                    es2 = es2_outer
                    osum = [sb(es2, "osum%d" % k, [128, 512], F32) for k in range(2)]
                    osq = [sb(es2, "osq%d" % k, [128, 512], BF16) for k in range(2)]
                    ors = [sb(es2, "ors%d" % k, [128, 512], F32) for k in range(2)]
                    def fin_gen(fi, hh, tb):
                        ch = 2 * p + hh
                        t0, n_ = TBS[tb]
                        k = fi % 2
                        okeys = [("OF", hh, t) for t in tiles if tb_of_tile(t) == tb]
                        act(osq[k][:, :n_], OF[:, hh, t0:t0 + n_], AF.Square, okeys, [("osq", k)])
                        yield
                        pn = ps_next()
                        mm(PS[pn][:, :n_], ones_bf[:], osq[k][:, :n_], True, True, ["ones_bf", ("osq", k)],
                           [("ps", pn)])
                        yield
                        rsqrt_from(ors[k][:, :n_], PS[pn][:, :n_], 1.0 / 128, [("ps", pn), "eps"], [("ors", k)])
                        yield
                        vstt(osum[k][:, :n_], OF[:, hh, t0:t0 + n_], gon[:, 0:1], ors[k][:, :n_], ALU.mult, ALU.mult,
                             okeys + ["gon", ("ors", k)], [("osum", k)])
                        vtt(MX[:, ch, t0:t0 + n_], osum[k][:, :n_], MX[:, ch, t0:t0 + n_], ALU.mult,
                            [("osum", k), ("MX", ch, tb)], [("MX", ch, tb)])

                    fgens = []
                    for hh in range(2):
                        for tb in tbs:
                            fgens.append(fin_gen(len(fgens), hh, tb))
                    run_staggered(fgens, width=2)
                    P.barrier()

    def chk(name):
        if stop == name:
            raise _Stop()

    def main_layers():
        with ExitStack() as es:
            adabuf = [sb(es, "adaw%d" % i, [128, 8, 512], BF16) for i in range(2)]
            for blk in range(4):
                mod_block(adabuf, 0, blk, MOD[0])
            mod_finish(0, MOD[0], which=(0,))
            norm_mod(norm_bufs(es, "nm"), 0, 0, MOD[0], [0, 1, 2, 3, 4])
            for blk in range(4, 12):
                mod_block(adabuf, 0, blk, MOD[0])
            mod_finish(0, MOD[0], which=(1,))
            P.barrier()
        chk("mod0")
        for l in range(n_layers):
            mod_t = MOD[l % 2]
            need_ctx = l < DEPTH - 1
            tbs_all = [0, 1, 2, 3, 4]
            tbs_res = tbs_all if need_ctx else [1, 2, 3, 4]
            tiles_all = list(range(NT))
            chk("norm0_%d" % l)
            if l % 2 == 0:
                with ExitStack() as es:
                    MX = sb(es, "MX", [128, 8, T], BF16)
                    tri = sb(es, "tri", [128, 6, 128], BF16)
                    A_T = sb(es, "A_T", [33, T], BF16)
                    wload(tri[:], "tri", "c_tri", tri_d[:, :, :])
                    memset("vector", A_T[32:33, :], 1.0, ["A_T1"])
                    sg_mixer(es, l, MX, tbs_all, tiles_all)
                    chk("sg_%d" % l)
                    gla_mixer(es, l, MX, tbs_all, tiles_all, tri, A_T)
                    chk("gla_%d" % l)
                    if dbg and ("mx%d" % l) in dbg_d:
                        P.barrier()
                        dump_fm(dbg_d["mx%d" % l], MX, 8, T)
                        P.barrier()
                    with ExitStack() as es2:
                        mix_out(es2, l, mod_t, MX, "MX", tbs_res)
                        P.barrier()
            else:
                odd_mixer(l, mod_t, tbs_all, tbs_res, need_ctx)
            chk("mix_%d" % l)
            with ExitStack() as es:
                nb = norm_bufs(es, "nm")
                norm_mod(nb, l, 1, mod_t, tbs_res)
                chk("norm1_%d" % l)
                nm = None
                if l + 1 < n_layers:
                    nm = (l + 1, MOD[(l + 1) % 2])
                ffn(es, l, mod_t, tbs_res, nm)
                if nm is not None:
                    mod_finish(l + 1, MOD[(l + 1) % 2])
                    norm_mod(nb, l + 1, 0, MOD[(l + 1) % 2], tbs_all)
                P.barrier()
            chk("ffn_%d" % l)

    h_dbg = [True]

    def odd_mixer(l, mod_t, tbs_all, tbs_res, need_ctx):
        i = l // 2
        MX = Z
        SC = 192.0 ** -0.5
        q_tbs = tbs_all if need_ctx else [1, 2, 3, 4]
        with ExitStack() as es:
            QAN = sb(es, "QAN", [128, 3, T], BF16)
            KVAN = sb(es, "KVAN", [128, 2, T], BF16)
            KPE = sb(es, "KPE", [64, T], BF16)
            qag = sb(es, "qag", [128, 3], F32)
            kvg = sb(es, "kvg", [128, 2], F32)
            qkn = sb(es, "qkn", [128, 4], F32)
            dma("sync", qag[:], qag_d[i, :, :], [], ["qag"], "c_qag")
            dma("sync", kvg[:], kvg_d[i, :, :], [], ["kvg"], "c_kvg")
            dma("sync", qkn[:], qkn_d[i, :, :], [], ["qkn"], "c_qkn")
            with ExitStack() as esA:
                FT = sb(esA, "FT", [128, 2, T], BF16)
                with ExitStack() as es1:
                    odw = sb(es1, "odw", [128, 8, 960], BF16)
                    sqt = [sb(es1, "sqt%d" % k, [128, 3, 512], BF16) for k in range(2)]
                    rsd = [sb(es1, "rsd%d" % k, [128, 512], F32) for k in range(2)]
                    for gi, (c0, c1) in enumerate(((0, 256), (256, 640), (640, 896), (896, 960))):
                        wload(odw[:, :, c0:c1], ("odw", gi), "odw%d" % gi,
                              odw_d[i, :, c0:c1].rearrange("(kc p) n -> p kc n", p=128))
                    def ft_gen(tb):
                        for j in range(2):
                            proj_fm(odw, ("odw", 0), j * 128, 128, tb, lambda pi, tb, t0, n_, j=j: vcopy(
                                FT[:, j, t0:t0 + n_], PS[pi][:, :n_], [("ps", pi)], [("FT", j, tb)]))
                        return
                        yield

                    def lat_gen(ci, tb, nm, nch, c0, gkey, gt, dstb, wk):
                        t0, n_ = TBS[tb]
                        sl = ci % 2
                        held = []
                        for c in range(nch):
                            pi = ps_next(hold=True)
                            held.append(pi)
                            for kc in range(8):
                                mm(PS[pi][:, :n_], odw[:, kc, c0 + c * 128:c0 + (c + 1) * 128], Z[:, kc, t0:t0 + n_],
                                   kc == 0, kc == 7, [("odw", wk), ("Z", kc, tb)], [("ps", pi)])
                            act(sqt[sl][:, c, :n_], PS[pi][:, :n_], AF.Square, [("ps", pi)], [("sqt", sl, c)])
                        yield
                        pss = ps_next()
                        for c in range(nch):
                            mm(PS[pss][:, :n_], ones_bf[:], sqt[sl][:, c, :n_], c == 0, c == nch - 1,
                               [("sqt", sl, c), "ones_bf"], [("ps", pss)])
                        rsqrt_from(rsd[sl][:, :n_], PS[pss][:, :n_], 1.0 / (128 * nch), [("ps", pss), "eps"],
                                   [("rsd", sl)])
                        yield
                        for c in range(nch):
                            vstt(dstb[:, c, t0:t0 + n_], PS[held[c]][:, :n_], gt[:, c:c + 1], rsd[sl][:, :n_], ALU.mult,
                                 ALU.mult, [("ps", held[c]), gkey, ("rsd", sl)], [(nm + "n", c, tb)])
                            ps_release(held[c])

                    def kpe_gen(tb):
                        def evk(pi, tb, t0, n_):
                            vcopy(KPE[:, t0:t0 + n_], PS[pi][:64, :n_], [("ps", pi)], [("KPE", tb)])
                        proj_fm(odw, ("odw", 3), 896, 64, tb, evk)
                        return
                        yield

                    gens = []
                    ci = 0
                    for tb in tbs_all:
                        gens.append(ft_gen(tb))
                        gens.append(lat_gen(ci, tb, "qa", 3, 256, "qag", qag, QAN, 1))
                        ci += 1
                        gens.append(lat_gen(ci, tb, "kva", 2, 640, "kvg", kvg, KVAN, 2))
                        ci += 1
                        gens.append(kpe_gen(tb))
                    run_staggered(gens, width=O1W)
                    P.barrier()
                with ExitStack() as es2:
                    c64 = sb(es2, "c64", [128, 256], BF16)
                    Y = sb(es2, "Y", [128, NT, 512], BF16)
                    dfb = [sb(es2, "dfb%d" % k, [128, 2, SEQ], BF16) for k in range(2)]
                    wload(c64[:], "c64", "c_c64", c64_d[:, :])
                    f_tiles = list(range(NT)) if need_ctx else list(range(2, NT))
                    for t in f_tiles:
                        tb = tb_of_tile(t)
                        pi = ps_next()
                        for j in range(2):
                            mm(PS[pi][:, j * 256:(j + 1) * 256], FT[:, j, t * 128:(t + 1) * 128], c64[:], True, True,
                               [("FT", j, tb), "c64"], [("ps", pi)])
                        sc = (64.0 * (CTX if t < 2 else SEQ)) ** -0.5
                        act(Y[:, t, :], PS[pi][:, :], AF.Copy, [("ps", pi)], [("Y", t)], scale=sc)
                    for lt in range(16):
                        b = lt % 2
                        for cs in range(2):
                            dma("sync", dfb[b][:, cs, :], dftL_d[lt, :, cs, :], [], [("dfb", b, cs)], "dfb%d_%d" % (b, cs))
                        for j in range(2):
                            for lb in range(4):
                                pi = j * 4 + lb
                                for cs in range(2):
                                    mm(PS[pi][:, :], Y[:, 2 + lt, j * 256 + cs * 128:j * 256 + (cs + 1) * 128],
                                       dfb[b][:, cs, lb * 512:(lb + 1) * 512], lt == 0 and cs == 0,
                                       lt == 15 and cs == 1, [("Y", 2 + lt), ("dfb", b, cs)], [("ps", pi)])
                    for j in range(2):
                        for lb in range(4):
                            pi = j * 4 + lb
                            dst = MX[:, j, 256 + lb * 512:256 + (lb + 1) * 512]
                            if lb % 2 == 0:
                                vcopy(dst, PS[pi][:, :], [("ps", pi)], [("Z", j, 1 + lb)])
                            else:
                                act(dst, PS[pi][:, :], AF.Copy, [("ps", pi)], [("Z", j, 1 + lb)])
                    if need_ctx:
                        for lt in range(2):
                            for cs in range(2):
                                dma("sync", dfb[lt][:, cs, 0:CTX], dftC_d[lt, :, cs, :], [], [("dfb", lt, cs)], "dfb%d_%d" % (lt, cs))
                        for j in range(2):
                            pi = ps_next()
                            for lt in range(2):
                                for cs in range(2):
                                    mm(PS[pi][:, :CTX], Y[:, lt, j * 256 + cs * 128:j * 256 + (cs + 1) * 128],
                                       dfb[lt][:, cs, 0:CTX], lt == 0 and cs == 0, lt == 1 and cs == 1,
                                       [("Y", lt), ("dfb", lt, cs)], [("ps", pi)])
                            vcopy(MX[:, j, 0:CTX], PS[pi][:, :CTX], [("ps", pi)], [("Z", j, 0)])
                    P.barrier()
            with ExitStack() as es3:
                rope = sb(es3, "rope", [64, 2, SEQ], BF16)
                rmat = sb(es3, "rmat", [64, 64], BF16)
                wload(rope[:], "rope", "c_rope", rope_d[:, :, :])
                wload(rmat[:], "rmat", "c_rmat", rmat_d[:, :])
                wq = [sb(es3, "wuq%d" % k, [128, 3, 192], BF16) for k in range(2)]
                wkv = [sb(es3, "wukv%d" % k, [128, 2, 256], BF16) for k in range(2)]
                QN = sb(es3, "QN", [128, T], BF16)
                QR = sb(es3, "QR", [128, T], BF16)
                KN = [sb(es3, "KN%d" % k, [128, T], BF16) for k in range(2)]
                KR = [sb(es3, "KR%d" % k, [128, T], BF16) for k in range(2)]
                Vt = [sb(es3, "Vt%d" % k, [128, NT, 128], BF16) for k in range(2)]
                sqn = [sb(es3, "sqn%d" % k, [128, 512], BF16) for k in range(2)]
                sqr = [sb(es3, "sqr%d" % k, [64, 512], BF16) for k in range(2)]
                rst = [sb(es3, "rst%d" % k, [128, 512], F32) for k in range(2)]
                tq = [sb(es3, "tq%d" % k, [64, 512], F32) for k in range(2)]
                tqb = [sb(es3, "tqb%d" % k, [64, 512], BF16) for k in range(2)]
                Pt = [sb(es3, "Pt%d" % k, [128, 512], BF16) for k in range(3)]
                rec = sb(es3, "rec", [128, 512], F32)
                if h_dbg[0]:
                    print("attention phase arena use:", arena["cur"] - a_base)
                    h_dbg[0] = False
                memset("vector", QR[64:128, :], 0.0, [("QR", tb) for tb in range(5)])
                for k in range(2):
                    memset("vector", KR[k][64:128, :], 0.0, [("KR", k, tb) for tb in range(5)])

                def load_w(h):
                    b = h % 2
                    wload(wq[b][:], ("wuq", b), "wuq%d" % b,
                          wuq_d[i, :, h * 192:(h + 1) * 192].rearrange("(kc p) n -> p kc n", p=128))
                    wload(wkv[b][:], ("wukv", b), "wukv%d" % b,
                          wukv_d[i, :, h * 256:(h + 1) * 256].rearrange("(kc p) n -> p kc n", p=128))

                def qk_chain(cn, h, tb, is_q, light=True):
                    b = h % 2
                    t0, n_ = TBS[tb]
                    sl = cn % 2
                    gcol = 0 if is_q else 2
                    if is_q:
                        w_, wkey, nkc, src, skey = wq[b], ("wuq", b), 3, QAN, "qan"
                        dN, dR, nkey, rkey = QN, QR, ("QN", tb), ("QR", tb)
                    else:
                        w_, wkey, nkc, src, skey = wkv[b], ("wukv", b), 2, KVAN, "kvan"
                        dN, dR, nkey, rkey = KN[b], KR[b], ("KN", b, tb), ("KR", b, tb)
                    pn_ = bg_alloc()
                    for kc in range(nkc):
                        mm(PS[pn_][:, :n_], w_[:, kc, 0:128], src[:, kc, t0:t0 + n_], kc == 0, kc == nkc - 1,
                           [wkey, (skey, kc, tb)], [("ps", pn_)])
                    pr_ = None
                    if is_q:
                        pr_ = bg_alloc()
                        for kc in range(3):
                            mm(PS[pr_][:64, :n_], w_[:, kc, 128:192], src[:, kc, t0:t0 + n_], kc == 0, kc == 2,
                               [wkey, (skey, kc, tb)], [("ps", pr_)])
                    yield
                    act(sqn[sl][:, :n_], PS[pn_][:, :n_], AF.Square, [("ps", pn_)], [("sqn", sl)])
                    if is_q:
                        act(sqr[sl][:, :n_], PS[pr_][:64, :n_], AF.Square, [("ps", pr_)], [("sqr", sl)])
                        if light:
                            act(tq[sl][:, :n_], PS[pr_][:64, :n_], AF.Copy, [("ps", pr_)], [("tq", sl)])
                            bg_free(pr_)
                    else:
                        act(sqr[sl][:, :n_], KPE[:, t0:t0 + n_], AF.Square, [("KPE", tb)], [("sqr", sl)])
                    yield
                    pss = bg_alloc()
                    mm(PS[pss][:, :n_], ones_bf[:], sqn[sl][:, :n_], True, False, [("sqn", sl), "ones_bf"],
                       [("ps", pss)])
                    mm(PS[pss][:, :n_], ones_bf[0:64, :], sqr[sl][:, :n_], False, True, [("sqr", sl), "ones_bf"],
                       [("ps", pss)])
                    yield
                    rsqrt_from(rst[sl][:, :n_], PS[pss][:, :n_], 1.0 / 192, [("ps", pss), "eps"], [("rst", sl)])
                    bg_free(pss)
                    yield
                    vstt(dN[:, t0:t0 + n_], PS[pn_][:, :n_], qkn[:, gcol:gcol + 1], rst[sl][:, :n_], ALU.mult,
                         ALU.mult, [("ps", pn_), "qkn", ("rst", sl)], [nkey])
                    bg_free(pn_)
                    if is_q:
                        if light:
                            vstt(tq[sl][:, :n_], tq[sl][:, :n_], qkn[0:64, 1:2], rst[sl][0:64, :n_], ALU.mult,
                                 ALU.mult, [("tq", sl), "qkn", ("rst", sl)], [("tq", sl)])
                        else:
                            vstt(tq[sl][:, :n_], PS[pr_][:64, :n_], qkn[0:64, 1:2], rst[sl][0:64, :n_], ALU.mult,
                                 ALU.mult, [("ps", pr_), "qkn", ("rst", sl)], [("tq", sl)])
                            bg_free(pr_)
                    else:
                        vstt(tq[sl][:, :n_], KPE[:, t0:t0 + n_], qkn[0:64, 3:4], rst[sl][0:64, :n_], ALU.mult,
                             ALU.mult, [("KPE", tb), "qkn", ("rst", sl)], [("tq", sl)])
                    dst = dR[0:64, t0:t0 + n_]
                    if tb == 0:
                        vcopy(dst, tq[sl][:, :n_], [("tq", sl)], [rkey])
                        return
                    vcopy(tqb[sl][:, :n_], tq[sl][:, :n_], [("tq", sl)], [("tqb", sl)])
                    yield
                    l0 = t0 - CTX
                    pr = bg_alloc()
                    mm(PS[pr][:64, :n_], rmat[:, :], tqb[sl][:, :n_], True, True, ["rmat", ("tqb", sl)], [("ps", pr)])
                    yield
                    vtt(tq[sl][:, :n_], tq[sl][:, :n_], rope[:, 0, l0:l0 + n_], ALU.mult, [("tq", sl), "rope"],
                        [("tq", sl)])
                    vtt(PS[pr][:64, :n_], PS[pr][:64, :n_], rope[:, 1, l0:l0 + n_], ALU.mult, [("ps", pr), "rope"],
                        [("ps", pr)])
                    vtt(dst, tq[sl][:, :n_], PS[pr][:64, :n_], ALU.add, [("tq", sl), ("ps", pr)], [rkey])
                    bg_free(pr)

                def v_chain(h, t):
                    b = h % 2
                    tb = tb_of_tile(t)
                    pv = bg_alloc()
                    for kc in range(2):
                        mm(PS[pv][:, 0:128], KVAN[:, kc, t * 128:(t + 1) * 128], wkv[b][:, kc, 128:256], kc == 0,
                           kc == 1, [("wukv", b), ("kvan", kc, tb)], [("ps", pv)])
                    yield
                    vcopy(Vt[b][:, t, :], PS[pv][:, 0:128], [("ps", pv)], [("Vt", b, t)])
                    bg_free(pv)

                bgp = {"banks": list(range(8)), "held": set(), "rr": 0}

                def bg_alloc():
                    nb = len(bgp["banks"])
                    for _ in range(nb):
                        i = bgp["banks"][bgp["rr"] % nb]
                        bgp["rr"] += 1
                        if i not in bgp["held"]:
                            bgp["held"].add(i)
                            return i
                    raise RuntimeError("background PSUM pool exhausted")

                def bg_free(i):
                    bgp["held"].discard(i)

                class BG:
                    def __init__(self, width):
                        self.q, self.active, self.width = [], [], width
                        self.tag, self.done = {}, set()

                    def add(self, g, tag=None):
                        self.q.append(g)
                        self.tag[id(g)] = tag

                    def ensure(self, pred):
                        while any(pred(self.tag[id(g)]) for g in self.q + self.active):
                            self.step()

                    def step(self):
                        for g in list(self.active):
                            try:
                                next(g)
                            except StopIteration:
                                self.active.remove(g)
                        if self.q and len(self.active) < self.width:
                            g = self.q.pop(0)
                            try:
                                next(g)
                                self.active.append(g)
                            except StopIteration:
                                pass

                    def drain(self):
                        while self.q or self.active:
                            self.step()

                bg = BG(2)
                cnq = [0]

                def add_k(h):
                    for tb in tbs_all:
                        bg.add(qk_chain(cnq[0], h, tb, False), ("kv", h))
                        cnq[0] += 1
                    for t in range(NT):
                        bg.add(v_chain(h, t), ("kv", h))

                def add_q(h, tb):
                    bg.add(qk_chain(cnq[0], h, tb, True, light=(h > 0)), ("q", h, tb))
                    cnq[0] += 1

                load_w(0)
                add_k(0)
                for tb in q_tbs:
                    add_q(0, tb)
                bg.width = 3
                bg.drain()
                bg.width = 2
                bgp["banks"] = [4, 5, 6, 7]
                bgp["held"].clear()
                bgp["rr"] = 0
                pn = 0
                for h in range(6):
                    b = h % 2
                    if h + 1 < 6:
                        load_w(h + 1)
                        add_k(h + 1)
                    bg.ensure(lambda tg, h=h: tg == ("kv", h))
                    for tb in q_tbs:
                        bg.ensure(lambda tg, h=h, tb=tb: tg == ("q", h, tb))
                        t0, n_ = TBS[tb]
                        kts = [0, 1] if tb == 0 else list(range(NT))
                        po, pd = 0, 1
                        sct = [0]

                        def s_mm(kt):
                            ktb = tb_of_tile(kt)
                            ks = slice(kt * 128, (kt + 1) * 128)
                            psc = 2 + sct[0] % 2
                            sct[0] += 1
                            mm(PS[psc][:, :n_], KN[b][:, ks], QN[:, t0:t0 + n_], True, False,
                               [("KN", b, ktb), ("QN", tb)], [("ps", psc)])
                            mm(PS[psc][:, :n_], KR[b][:, ks], QR[:, t0:t0 + n_], False, True,
                               [("KR", b, ktb), ("QR", tb)], [("ps", psc)])
                            return psc
                        psc_next = s_mm(kts[0])
                        for n, kt in enumerate(kts):
                            psc = psc_next
                            if n + 1 < len(kts):
                                psc_next = s_mm(kts[n + 1])
                            pt = Pt[pn % 3]
                            pkey = ("Pt", pn % 3)
                            pn += 1
                            act(pt[:, :n_], PS[psc][:, :n_], AF.Exp, [("ps", psc)], [pkey], scale=SC)
                            mm(PS[po][:, :n_], Vt[b][:, kt, :], pt[:, :n_], n == 0, n == len(kts) - 1,
                               [("Vt", b, kt), pkey], [("ps", po)])
                            mm(PS[pd][:, :n_], ones_bf[:], pt[:, :n_], n == 0, n == len(kts) - 1, ["ones_bf", pkey],
                               [("ps", pd)])
                            bg.step()
                        recip_act(rec[:, :n_], PS[pd][:, :n_], [("ps", pd)], ["rec"])
                        vtt(MX[:, 2 + h, t0:t0 + n_], PS[po][:, :n_], rec[:, :n_], ALU.mult, [("ps", po), "rec"],
                            [("Z", 2 + h, tb)])
                        if h + 1 < 6:
                            add_q(h + 1, tb)
                bg.drain()
                P.barrier()
            chk("attn_%d" % l)
            if dbg and ("mx%d" % l) in dbg_d:
                P.barrier()
                dump_fm(dbg_d["mx%d" % l], MX, 8, T)
                P.barrier()
            with ExitStack() as es4:
                mix_out(es4, l, mod_t, MX, "Z", tbs_res)
                P.barrier()

    main_mark = arena["cur"]
    try:
        main_layers()
    except _Stop:
        arena["cur"] = main_mark
        ps_held.clear()
        P.barrier()

    with ExitStack() as es:
        xo = [sb(es, "xo%d" % i, [128, D], F32) for i in range(2)]
        for n, t in enumerate(range(2, NT)):
            b = n % 2
            tb = tb_of_tile(t)
            for half in range(2):
                pi = ps_next()
                for q in range(4):
                    kc = half * 4 + q
                    tr(PS[pi][:, q * 128:(q + 1) * 128], ST[:, kc, t * 128:(t + 1) * 128], [("ST", kc, tb)],
                       [("ps", pi)])
                if half == 0:
                    vcopy(xo[b][:, 0:512], PS[pi][:, :], [("ps", pi)], [("xo", b, 0)])
                else:
                    act(xo[b][:, 512:1024], PS[pi][:, :], AF.Copy, [("ps", pi)], [("xo", b, 1)])
            dma("sync", out_d[(t - 2) * 128:(t - 1) * 128, :], xo[b][:], [("xo", b, 0), ("xo", b, 1)], [], "out%d" % b)
    if dbg and "st" in dbg_d:
        P.barrier()
        dump_fm(dbg_d["st"], ST, 8, T)
    if dbg and "z" in dbg_d:
        P.barrier()
        dump_fm(dbg_d["z"], Z, 8, T)
    P.emit(["out0", "out1"] + list(dbg_keys))
    top.close()
    print("SBUF arena peak bytes/partition:", arena["peak"], "of", a_size, "at", arena.get("peak_name"), "ops:", len(P.ops))
    return nc


def _consts():
    s = np.arange(128)
    same = (s[:, None] // 64) == (s[None, :] // 64)
    maskF = (same & (s[:, None] <= s[None, :])).astype(np.float32)
    maskB = (same & (s[:, None] >= s[None, :])).astype(np.float32)
    strictF = (same & (s[:, None] > s[None, :])).astype(np.float32)
    strictB = (same & (s[:, None] < s[None, :])).astype(np.float32)
    tri = np.stack([maskF * (-1.0 / 16), strictF * (-1.0 / 16), maskF,
                    maskB * (-1.0 / 16), strictB * (-1.0 / 16), maskB], axis=1).astype(np.float32)
    rows = SEQ // 64
    row_id = np.repeat(np.arange(rows, dtype=np.float64), 64)
    col_id = np.tile(np.arange(64, dtype=np.float64), rows)
    inv = 10000.0 ** (-np.arange(0, 32, 2, dtype=np.float64) / 32)
    ang_r = row_id[:, None] * inv
    ang_c = col_id[:, None] * inv
    ang = np.concatenate([ang_r, ang_r, ang_c, ang_c], axis=-1)
    rope = np.stack([np.cos(ang).T, np.sin(ang).T], axis=1).astype(np.float32)
    rmat = np.zeros((64, 64), np.float32)
    for blk in range(2):
        for q in range(16):
            a, b = blk * 32 + q, blk * 32 + 16 + q
            rmat[b, a] = -1.0
            rmat[a, b] = 1.0
    k64 = np.arange(64)
    a64 = 2 * np.pi * ((k64[:, None] * k64[None, :]) % 64) / 64
    c64, s64 = np.cos(a64), np.sin(a64)
    z = np.zeros((64, 64))
    c64blk = np.concatenate([np.block([[c64, z], [z, c64]]), np.block([[-s64, z], [z, -s64]])], axis=1).astype(np.float32)

    def dft(L):
        li = np.arange(L, dtype=np.int64)
        a = 2 * np.pi * ((li[:, None] * li[None, :]) % L).astype(np.float64) / L
        m = np.stack([np.cos(a), np.sin(a)], axis=1).astype(np.float32)
        return np.ascontiguousarray(m.reshape(L // 128, 128, 2, L)).astype(ml_dtypes.bfloat16)

    return dict(ident=np.eye(128, dtype=np.float32), tri=tri, rope_cs=rope, rmat=rmat, c64blk=c64blk,
                dft2048=dft(SEQ), dft256=dft(CTX))


def _fm(v, nchunk):
    return np.ascontiguousarray(np.asarray(v, np.float32).reshape(nchunk, 128).T)


def prep_inputs(inp, cores):
    g = {k: np.asarray(v) for k, v in inp.items()}
    shared = dict(_consts())
    shared["ada_w"] = g["ada_w"]
    shared["ada_b_fm"] = np.stack([_fm(g["ada_b"][l], 48) for l in range(DEPTH)])
    shared["ng_fm"] = np.stack([np.concatenate([_fm(g["norm_mix_g"][l], 8), _fm(g["norm_ffn_g"][l], 8)], axis=1)
                                for l in range(DEPTH)])
    for k in ("w_mix_out", "ffn_w_in", "ffn_w_out", "ev_w_in", "od_w_in", "mla_wuq", "mla_wukv"):
        shared[k] = g[k]
    wab = np.zeros((2, 33, 2, 256), np.float32)
    for i in range(2):
        for p in range(2):
            cs = slice(p * 128, (p + 1) * 128)
            wab[i, 0:16, p, 0:128] = g["gla_wa_f"][i][:, cs]
            wab[i, 16:32, p, 128:256] = g["gla_wa_b"][i][:, cs]
            wab[i, 32, p, 0:128] = g["gla_ba_f"][i][cs]
            wab[i, 32, p, 128:256] = g["gla_ba_b"][i][cs]
    shared["gla_wab"] = wab
    shared["gla_on_fm"] = np.ascontiguousarray(g["gla_onorm_g"].reshape(2, 128, 1))
    shared["sg_vng_bc"] = np.ascontiguousarray(np.broadcast_to(g["sg_vnorm_g"].reshape(2, 1, 512), (2, 128, 512)))
    shared["sg_wsT"] = np.ascontiguousarray(np.transpose(g["sg_ws"], (0, 3, 1, 2)))
    shared["sg_bs"] = np.ascontiguousarray(g["sg_bs"].reshape(2, 1, 512))
    shared["qa_g_fm"] = np.stack([_fm(g["mla_qa_g"][i], 3) for i in range(2)])
    shared["kva_g_fm"] = np.stack([_fm(g["mla_kva_g"][i], 2) for i in range(2)])
    qkn = np.zeros((2, 128, 4), np.float32)
    for i in range(2):
        qkn[i, :, 0] = g["mla_qn_g"][i][:128]
        qkn[i, :64, 1] = g["mla_qn_g"][i][128:]
        qkn[i, :, 2] = g["mla_kn_g"][i][:128]
        qkn[i, :64, 3] = g["mla_kn_g"][i][128:]
    shared["qkn_g_fm"] = qkn
    maps = []
    for b in cores:
        m = dict(shared)
        m["x"] = np.ascontiguousarray(g["x"][b])
        m["ctx"] = np.ascontiguousarray(g["ctx"][b])
        m["cvec"] = np.ascontiguousarray(np.stack([_fm(g["c"][b], 8), _fm(g["c_ctx"], 8)], axis=-1))
        maps.append(m)
    return maps


def kernel(**inputs):
    nc = build_program()
    maps = prep_inputs(inputs, list(range(8)))
    res = run_bass_kernel_spmd(nc, maps, core_ids=list(range(8)))
    return np.stack([np.asarray(r["out"], np.float32) for r in res.results], axis=0)
```
